# Optimizing a Trainium2 kernel written in Bass

```python
import math
import jax, jax.numpy as jnp
from jax import lax
import numpy as np

D_MODEL = 1024
BATCH = 16
SEQ = 2048
DEPTH = 1

HEAD_DIM = 64
ATTN_GROUPS = ((128, 1), (512, 4), (2048, 16))
ATTN_WIDTH = 3 * D_MODEL // 4
N_ATTN_HEADS = ATTN_WIDTH // HEAD_DIM
HEADS_PER_GROUP = N_ATTN_HEADS // len(ATTN_GROUPS)
ATTN_OUT_WIDTH = HEADS_PER_GROUP * HEAD_DIM
FOURIER_WIDTH = D_MODEL // 4
FOURIER_GROUPS = 4
FOURIER_GROUP_DIM = FOURIER_WIDTH // FOURIER_GROUPS
ROPE_THETA = 500000.0
ROPE_DIM = HEAD_DIM // 4
MEM_LEN = 256
CROSS_HEADS = 4
CROSS_HEAD_DIM = D_MODEL // CROSS_HEADS
D_FF = 4 * D_MODEL
QUERY_BLOCK = 64
EPS = 1e-6
IN_WIDTH = FOURIER_WIDTH + 3 * ATTN_WIDTH + 2 * D_MODEL
NEG_BIG = -1e30

kernel_name = 'hybrid_fourier_dilated_encoder_block'


def rms_norm(x, g):
    xf = x.astype(jnp.float32)
    y = xf * lax.rsqrt(jnp.mean(xf * xf, axis=-1, keepdims=True) + EPS)
    return (y * g.astype(jnp.float32)).astype(x.dtype)


def partial_rope(t):
    S = t.shape[1]
    half = ROPE_DIM // 2
    inv_freq = ROPE_THETA ** (-jnp.arange(half, dtype=jnp.float32) / half)
    ang = jnp.arange(S, dtype=jnp.float32)[:, None] * inv_freq[None, :]
    cos = jnp.cos(ang)[None, :, None, :]
    sin = jnp.sin(ang)[None, :, None, :]
    tr = t[..., :ROPE_DIM].astype(jnp.float32)
    t1, t2 = tr[..., :half], tr[..., half:]
    rot = jnp.concatenate([t1 * cos - t2 * sin, t1 * sin + t2 * cos], axis=-1).astype(t.dtype)
    return jnp.concatenate([rot, t[..., ROPE_DIM:]], axis=-1)


def dilated_window_attention(q, k, v, window, dilation):
    B, S, H, Dh = q.shape
    L = S // dilation
    reach = window // (2 * dilation)
    qb = math.gcd(L, QUERY_BLOCK)
    nblk = L // qb
    kw = qb + 2 * reach

    def stride_split(t):
        return t.reshape(B, L, dilation, H, Dh).transpose(0, 3, 2, 1, 4)

    qs, ks, vs = stride_split(q), stride_split(k), stride_split(v)
    pad = ((0, 0), (0, 0), (0, 0), (reach, reach), (0, 0))
    kp = jnp.pad(ks, pad)
    vp = jnp.pad(vs, pad)
    key_idx = jnp.arange(nblk)[:, None] * qb + jnp.arange(kw)[None, :]
    kb = kp[:, :, :, key_idx].astype(jnp.float32)
    vb = vp[:, :, :, key_idx].astype(jnp.float32)
    qblk = qs.reshape(B, H, dilation, nblk, qb, Dh).astype(jnp.float32)
    s = jnp.einsum('bhrnqe,bhrnke->bhrnqk', qblk, kb) * (Dh ** -0.5)
    qpos = jnp.arange(nblk)[:, None] * qb + jnp.arange(qb)[None, :]
    kpos = key_idx - reach
    rel = kpos[:, None, :] - qpos[:, :, None]
    valid = (jnp.abs(rel) <= reach) & (kpos[:, None, :] >= 0) & (kpos[:, None, :] < L)
    s = jnp.where(valid, s, NEG_BIG)
    m = jnp.max(s, axis=-1, keepdims=True)
    p = jnp.exp(s - m)
    denom = jnp.sum(p, axis=-1, keepdims=True)
    o = jnp.einsum('bhrnqk,bhrnke->bhrnqe', p / denom, vb)
    lse = (m + jnp.log(denom))[..., 0]
    o = o.reshape(B, H, dilation, L, Dh).transpose(0, 3, 2, 1, 4).reshape(B, S, H, Dh)
    lse = lse.reshape(B, H, dilation, L).transpose(0, 3, 2, 1).reshape(B, S, H)
    return o, lse


def hybrid_mixer(h, w_in, fourier_mix, w_fourier_branch, w_attn_branch, w_mix_out):
    B, S, _ = h.shape
    proj = h @ w_in
    c0 = FOURIER_WIDTH
    c1 = c0 + ATTN_WIDTH
    c2 = c1 + ATTN_WIDTH
    c3 = c2 + ATTN_WIDTH
    c4 = c3 + D_MODEL
    u, q, k, v, g_f, g_a = jnp.split(proj, [c0, c1, c2, c3, c4], axis=-1)

    uf = u.reshape(B, S, FOURIER_GROUPS, FOURIER_GROUP_DIM).astype(jnp.float32)
    spec = jnp.fft.fft2(uf, axes=(1, 3), norm='ortho').real.astype(h.dtype)
    y_f = jnp.einsum('bsgc,gcd->bsgd', spec, fourier_mix).reshape(B, S, FOURIER_WIDTH)

    q = partial_rope(q.reshape(B, S, N_ATTN_HEADS, HEAD_DIM))
    k = partial_rope(k.reshape(B, S, N_ATTN_HEADS, HEAD_DIM))
    v = v.reshape(B, S, N_ATTN_HEADS, HEAD_DIM)
    outs, lses = [], []
    for gi, (window, dilation) in enumerate(ATTN_GROUPS):
        sl = slice(gi * HEADS_PER_GROUP, (gi + 1) * HEADS_PER_GROUP)
        o, lse = dilated_window_attention(q[:, :, sl], k[:, :, sl], v[:, :, sl], window, dilation)
        outs.append(o)
        lses.append(lse)
    o_all = jnp.stack(outs, axis=0)
    lse_all = jnp.stack(lses, axis=0)
    wts = jax.nn.softmax(lse_all, axis=0)
    y_a = jnp.sum(wts[..., None] * o_all, axis=0).astype(h.dtype).reshape(B, S, ATTN_OUT_WIDTH)

    merged = jax.nn.sigmoid(g_f) * (y_f @ w_fourier_branch) + jax.nn.sigmoid(g_a) * (y_a @ w_attn_branch)
    return merged @ w_mix_out


def memory_cross_attention(h, mem_n, w_cq, w_ckv, w_co):
    B, S, _ = h.shape
    M = mem_n.shape[1]
    q = (h @ w_cq).reshape(B, S, CROSS_HEADS, CROSS_HEAD_DIM)
    k, v = jnp.split(mem_n @ w_ckv, 2, axis=-1)
    k = k.reshape(B, M, CROSS_HEADS, CROSS_HEAD_DIM)
    v = v.reshape(B, M, CROSS_HEADS, CROSS_HEAD_DIM)
    s = jnp.einsum('bshe,bmhe->bhsm', q.astype(jnp.float32), k.astype(jnp.float32)) * (CROSS_HEAD_DIM ** -0.5)
    p = jax.nn.softmax(s, axis=-1)
    o = jnp.einsum('bhsm,bmhe->bshe', p, v.astype(jnp.float32)).astype(h.dtype).reshape(B, S, D_MODEL)
    return o @ w_co


def sq_relu_mlp(h, w_up, w_down):
    a = jax.nn.relu(h @ w_up)
    return (a * a) @ w_down


def setup_inputs(seed: int = 0) -> dict:
    key = jax.random.key(seed)
    ks = jax.random.split(key, 20)
    f32 = jnp.float32

    def nrm(k, shape, fan_in):
        return jax.random.normal(k, shape, f32) * (fan_in ** -0.5)

    def gain(k, shape):
        return 1.0 + 0.01 * jax.random.normal(k, shape, f32)

    return {
        'x': jax.random.normal(ks[0], (BATCH, SEQ, D_MODEL), f32),
        'mem': jax.random.normal(ks[1], (BATCH, MEM_LEN, D_MODEL), f32),
        'norm_mix': gain(ks[2], (DEPTH, D_MODEL)),
        'w_in': nrm(ks[3], (DEPTH, D_MODEL, IN_WIDTH), D_MODEL),
        'fourier_mix': nrm(ks[4], (DEPTH, FOURIER_GROUPS, FOURIER_GROUP_DIM, FOURIER_GROUP_DIM), FOURIER_GROUP_DIM),
        'w_fourier_branch': nrm(ks[5], (DEPTH, FOURIER_WIDTH, D_MODEL), FOURIER_WIDTH),
        'w_attn_branch': nrm(ks[6], (DEPTH, ATTN_OUT_WIDTH, D_MODEL), ATTN_OUT_WIDTH),
        'w_mix_out': nrm(ks[7], (DEPTH, D_MODEL, D_MODEL), D_MODEL),
        'norm_cross': gain(ks[8], (DEPTH, D_MODEL)),
        'norm_mem': gain(ks[9], (DEPTH, D_MODEL)),
        'w_cq': nrm(ks[10], (DEPTH, D_MODEL, D_MODEL), D_MODEL),
        'w_ckv': nrm(ks[11], (DEPTH, D_MODEL, 2 * D_MODEL), D_MODEL),
        'w_co': nrm(ks[12], (DEPTH, D_MODEL, D_MODEL), D_MODEL),
        'norm_mlp': gain(ks[13], (DEPTH, D_MODEL)),
        'w_up': nrm(ks[14], (DEPTH, D_MODEL, D_FF), D_MODEL),
        'w_down': nrm(ks[15], (DEPTH, D_FF, D_MODEL), D_FF),
        'norm_final': gain(ks[16], (D_MODEL,)),
    }


def reference(x, mem, norm_mix, w_in, fourier_mix, w_fourier_branch, w_attn_branch, w_mix_out,
              norm_cross, norm_mem, w_cq, w_ckv, w_co, norm_mlp, w_up, w_down, norm_final):
    for l in range(DEPTH):
        h = rms_norm(x, norm_mix[l])
        x = x + hybrid_mixer(h, w_in[l], fourier_mix[l], w_fourier_branch[l], w_attn_branch[l], w_mix_out[l])
        h = rms_norm(x, norm_cross[l])
        mem_n = rms_norm(mem, norm_mem[l])
        x = x + memory_cross_attention(h, mem_n, w_cq[l], w_ckv[l], w_co[l])
        h = rms_norm(x, norm_mlp[l])
        x = x + sq_relu_mlp(h, w_up[l], w_down[l])
    return rms_norm(x, norm_final)
```

```python
import numpy as np
import ml_dtypes
import concourse.bass as bass
import concourse.mybir as mybir
from concourse.bass_utils import run_bass_kernel_spmd

F32 = mybir.dt.float32
BF16 = mybir.dt.bfloat16
AF = mybir.ActivationFunctionType
ALU = mybir.AluOpType
DSZ = {F32: 4, BF16: 2}

S = 2048
D = 1024
NT = 16
NTG = 4
NEG = -30000.0
N_CORES = 8
SEQ_PER_CORE = 2


class Prog:
    BLK = 256

    def __init__(self, nc):
        self.nc = nc
        self.ops = []
        self.lastw = {}
        self.readers = {}
        self.dma_prev = {}
        self.nbank = 0

    def res(self, ap):
        sp = str(ap.space)
        if 'SB' in sp:
            kind = 'sb'
        elif 'PS' in sp.upper():
            kind = 'ps'
        else:
            return []
        pat = ap.ap
        pstep = pat[0][0]
        free_off = ap.offset - ap.base_partition() * pstep
        sz = DSZ[ap.dtype]
        dims = [(s, c) for (s, c) in pat[1:]]
        if not dims:
            dims = [(1, 1)]
        inner_s, inner_c = dims[-1]
        outer = dims[:-1]
        nouter = 1
        for s, c in outer:
            nouter *= c
        ivals = []
        if nouter <= 512:
            starts = [0]
            for s, c in outer:
                starts = [b + i * s for b in starts for i in range(c)]
            for st in starts:
                lo = free_off + st
                if inner_s < 0:
                    lo -= (inner_c - 1) * (-inner_s)
                hi = lo + (inner_c - 1) * abs(inner_s) + 1
                ivals.append((lo * sz, hi * sz))
        else:
            lo = free_off
            hi = free_off + sum((c - 1) * abs(s) for s, c in dims) + 1
            ivals.append((lo * sz, hi * sz))
        out = set()
        if kind == 'ps':
            for lo, hi in ivals:
                for b in range(lo // 2048, (hi - 1) // 2048 + 1):
                    out.add(('ps', b))
        else:
            for lo, hi in ivals:
                for b in range(lo // self.BLK, (hi - 1) // self.BLK + 1):
                    out.add(b)
        return out

    def op(self, eng, name, *args, reads=(), writes=(), dma_key=None, **kw):
        i = len(self.ops)
        rset, wset = set(), set()
        for a in reads:
            r = self.res(a)
            for k in r:
                if isinstance(k, tuple):
                    wset.add(k)
                else:
                    rset.add(k)
        for a in writes:
            wset |= set(self.res(a))
        raw, oth = set(), set()
        for k in rset:
            j = self.lastw.get(k)
            if j is not None:
                raw.add(j)
        for k in wset:
            j = self.lastw.get(k)
            if j is not None:
                (raw if isinstance(k, tuple) else oth).add(j)
            for j in self.readers.get(k, ()):
                oth.add(j)
        if dma_key is not None:
            j = self.dma_prev.get(dma_key)
            if j is not None:
                raw.add(j)
            self.dma_prev[dma_key] = i
        for k in rset:
            self.readers.setdefault(k, []).append(i)
        for k in wset:
            self.lastw[k] = i
            self.readers[k] = []
        deps = set()
        for j in raw | oth:
            oj = self.ops[j]
            if oj['dma'] is None and oj['eng'] == eng:
                if eng == 'pe':
                    continue
            deps.add(j)
        self.ops.append(dict(eng=eng, name=name, args=args, kw=kw, deps=deps, dma=dma_key))
        return i

    def bank(self):
        b = self.nbank % 8
        self.nbank += 1
        return b

    def emit(self, sem_alloc):
        nc = self.nc
        engs = {'pe': nc.tensor, 'act': nc.scalar, 'dve': nc.vector, 'pool': nc.gpsimd, 'sp': nc.sync}
        n = len(self.ops)
        need = [False] * n
        for o in self.ops:
            for j in o['deps']:
                need[j] = True
        esem = {e: sem_alloc('e_' + e) for e in ('pe', 'act', 'dve', 'pool')}
        dsem = {}
        ecnt = {e: 0 for e in esem}
        dcnt = {}
        sig = [None] * n
        for i, o in enumerate(self.ops):
            if o['dma'] is not None:
                k = o['dma']
                if k not in dsem:
                    dsem[k] = sem_alloc('d_' + k)
                    dcnt[k] = 0
                dcnt[k] += 16
                sig[i] = (dsem[k], dcnt[k])
            elif need[i]:
                e = o['eng']
                ecnt[e] += 1
                sig[i] = (esem[e], ecnt[e])
        seen = {e: {} for e in engs}
        for i, o in enumerate(self.ops):
            e = o['eng']
            eo = engs[e]
            want = {}
            for j in o['deps']:
                sm, v = sig[j]
                if want.get(sm.num, (None, 0))[1] < v:
                    want[sm.num] = (sm, v)
            for num, (sm, v) in sorted(want.items()):
                if seen[e].get(num, 0) >= v:
                    continue
                eo.wait_ge(sm, v)
                seen[e][num] = v
            ins = getattr(eo, o['name'])(*o['args'], **o['kw'])
            if sig[i] is not None:
                ins.then_inc(sig[i][0], 16 if o['dma'] is not None else 1)
        for k, sm in dsem.items():
            if seen['sp'].get(sm.num, 0) < dcnt[k]:
                nc.sync.wait_ge(sm, dcnt[k])


def host_constants():
    bf = ml_dtypes.bfloat16
    c = {}
    c['ident'] = np.eye(128, dtype=np.float32).astype(bf)
    c['ones'] = np.ones((128, 128), np.float32).astype(bf)
    perm = np.zeros((128, 128), np.float32)
    for p in range(128):
        e = p % 64
        if e < 8:
            perm[p + 8, p] = 1.0
        elif e < 16:
            perm[p - 8, p] = 1.0
    c['perm'] = perm.astype(bf)
    kp = np.arange(128)[:, None]
    q = np.arange(128)[None, :]
    masks = np.stack([
        np.where(q - kp >= 64, 1.0, 0.0),
        np.where(np.abs(kp - q) <= 64, 1.0, 0.0),
        np.where(kp - q >= 64, 1.0, 0.0),
    ]).astype(np.float32)
    c['masks'] = masks.astype(bf)
    i64 = np.arange(64)
    ang = 2 * np.pi * np.outer(i64, i64) / 64.0
    bdc = np.zeros((128, 128)); bds = np.zeros((128, 128))
    for g in range(2):
        bdc[g * 64:(g + 1) * 64, g * 64:(g + 1) * 64] = np.cos(ang)
        bds[g * 64:(g + 1) * 64, g * 64:(g + 1) * 64] = np.sin(ang)
    c['bdcos'] = bdc.astype(np.float32).astype(bf)
    c['bdsin'] = bds.astype(np.float32).astype(bf)
    s = np.arange(S, dtype=np.int64)
    ph = (np.outer(s, s) % S).astype(np.float64) * (2 * np.pi / S)
    nrm = 1.0 / np.sqrt(S * 64.0)
    c['dcos'] = (np.cos(ph) * nrm).astype(np.float32).astype(bf)
    c['dnsin'] = (-np.sin(ph) * nrm).astype(np.float32).astype(bf)
    half = 8
    inv_freq = (np.float32(500000.0) ** (-np.arange(half, dtype=np.float32) / np.float32(half))).astype(np.float32)
    angr = (np.arange(S, dtype=np.float32)[:, None] * inv_freq[None, :]).astype(np.float32)
    cosr = np.cos(angr).astype(np.float32).T
    sinr = np.sin(angr).astype(np.float32).T
    rc = np.ones((128, S), np.float32)
    rs = np.zeros((128, S), np.float32)
    for p in range(128):
        e = p % 64
        if e < 8:
            rc[p] = cosr[e]; rs[p] = -sinr[e]
        elif e < 16:
            rc[p] = cosr[e - 8]; rs[p] = sinr[e - 8]
    c['ropec'] = rc.astype(bf)
    c['ropes'] = rs.astype(bf)
    c['alt'] = (((-1.0) ** np.arange(128))[:, None] * nrm).astype(np.float32).astype(bf)
    sel = np.zeros((128, 64), np.float32)
    sel[64, :] = 1.0
    c['sel'] = sel
    return c


CONST_SPECS = [
    ('ident', [128, 128], BF16), ('ones', [128, 128], BF16), ('perm', [128, 128], BF16),
    ('masks', [3, 128, 128], BF16), ('bdcos', [128, 128], BF16), ('bdsin', [128, 128], BF16),
    ('dcos', [S, S], BF16), ('dnsin', [S, S], BF16),
    ('ropec', [128, S], BF16), ('ropes', [128, S], BF16), ('sel', [128, 64], F32), ('alt', [128, 1], BF16),
]
W_SPECS = [
    ('norm_mix', [D]), ('w_in', [D, 4608]), ('fourier_mix', [4, 64, 64]), ('w_fourier_branch', [256, D]),
    ('w_attn_branch', [256, D]), ('w_mix_out', [D, D]), ('norm_cross', [D]), ('norm_mem', [D]),
    ('w_cq', [D, D]), ('w_ckv', [D, 2 * D]), ('w_co', [D, D]), ('norm_mlp', [D]),
    ('w_up', [D, 4 * D]), ('w_down', [4 * D, D]), ('norm_final', [D]),
]


def build_program(nseq, debug=False):
    nc = bass.Bass("TRN2", target_bir_lowering=False)
    dr = {}
    dr['x'] = nc.dram_tensor("x", [nseq, S, D], F32, kind="ExternalInput").ap()
    dr['mem'] = nc.dram_tensor("mem", [nseq, 256, D], F32, kind="ExternalInput").ap()
    for name, shp in W_SPECS:
        dr[name] = nc.dram_tensor(name, shp, F32, kind="ExternalInput").ap()
    for name, shp, dt in CONST_SPECS:
        dr[name] = nc.dram_tensor(name, shp, dt, kind="ExternalInput").ap()
    out_d = nc.dram_tensor("out", [nseq, S, D], F32, kind="ExternalOutput").ap()
    dbg = {}
    if debug:
        for nm, shp, dt in (('d_hT', [128, 8, 2048], BF16), ('d_yfT', [128, 2, 2048], BF16), ('d_yaT', [128, 4, 2048], BF16),
                            ('d_acc', [128, 4, 2048], F32), ('d_x1', [128, 16, 1024], F32), ('d_x2', [128, 16, 1024], F32),
                            ('d_q', [128, 2, 2048], BF16), ('d_k', [128, 2, 2048], BF16), ('d_v', [128, 16, 4, 80], BF16)):
            dbg[nm] = nc.dram_tensor(nm, shp, dt, kind="ExternalOutput").ap()

    OFF = {}
    cur = [0]

    def alloc(name, n):
        n = (n + 127) // 128 * 128
        OFF[name] = cur[0]
        cur[0] += n
        return OFF[name]

    alloc('const', 2816)
    alloc('memnT', 2048)
    alloc('wring', 5 * 4096)
    alloc('hT', 8 * 2048)
    alloc('temps', 4096)
    alloc('yfT', 2 * 2048)
    alloc('yaT', 4 * 2048)
    alloc('X', 16 * 1024 * 2)
    alloc('U', 15360)
    TOTAL = cur[0]

    sems = []
    import contextlib
    with contextlib.ExitStack() as es:
        A = es.enter_context(nc.sbuf_tensor("arena", [128, TOTAL], BF16))
        PSt = es.enter_context(nc.psum_tensor("ps", [128, 8, 512], F32))

        def sem_alloc(name):
            s_ = es.enter_context(nc.semaphore(name))
            sems.append(s_)
            return s_

        P = Prog(nc)

        def view(off, cnt_, dt=BF16, pat=None, **kw):
            v = A[:, off:off + cnt_]
            if dt == F32:
                v = v.bitcast(F32)
            if pat:
                v = v.rearrange(pat, **kw)
            return v

        def psf(b):
            return PSt[:, b, :]

        def psb(b):
            return PSt[:, b, :].bitcast(BF16)

        OP = P.op

        def mm(out, lhsT, rhs, start, stop, sgc=False):
            if sgc:
                OP('pe', 'matmul', out, lhsT, rhs, start=start, stop=stop, skip_group_check=True, reads=[lhsT, rhs], writes=[out])
            else:
                OP('pe', 'matmul', out, lhsT, rhs, start=start, stop=stop, reads=[lhsT, rhs], writes=[out])

        def dma(q, out, in_, key):
            OP(q, 'dma_start', out=out, in_=in_, reads=[in_], writes=[out], dma_key=key)

        c0 = OFF['const']
        ident = view(c0 + 0, 128)
        ones = view(c0 + 128, 128)
        perm = view(c0 + 256, 128)
        masks = view(c0 + 384, 384, pat="p (m q) -> p m q", m=3)
        bdcos = view(c0 + 768, 128)
        bdsin = view(c0 + 896, 128)
        Wz = view(c0 + 1024, 512, pat="p (j c) -> p j c", j=2)
        sel = view(c0 + 1536, 128, F32)
        gT = view(c0 + 1664, 128, F32, pat="p (n c) -> p n c", n=8)
        negh = view(c0 + 1792, 32, F32)
        ST = view(c0 + 1920, 384, F32)
        fmst = view(c0 + 2304, 128, pat="p (j d) -> p j d", j=2)
        fm32 = view(c0 + 2432, 256, F32, pat="p (j d) -> p j d", j=2)

        dma('sp', ident, dr['ident'], 'c')
        dma('sp', ones, dr['ones'], 'c1')
        dma('sp', perm, dr['perm'], 'c2')
        dma('sp', masks, dr['masks'].rearrange("m p q -> p m q"), 'c3')
        dma('sp', bdcos, dr['bdcos'], 'c4')
        dma('sp', bdsin, dr['bdsin'], 'c5')
        dma('sp', sel, dr['sel'], 'c6')
        altc = view(c0 + 2688, 2)[:, 0:1]
        dma('sp', altc, dr['alt'], 'c7')
        GIDX = {'norm_mix': 0, 'norm_cross': 1, 'norm_mem': 2, 'norm_mlp': 3}
        for nm, gi in GIDX.items():
            OP('pool', 'dma_start', out=gT[:, gi, :], in_=dr[nm].rearrange("(c p) -> p c", p=128),
               reads=[], writes=[gT[:, gi, :]], dma_key='g%d' % gi, allow_slow_non_contiguous=True)
        OP('dve', 'memset', negh, -0.5, writes=[negh])
        OP('dve', 'memset', negh[:, 1:2], 1e-6, writes=[negh[:, 1:2]])
        OP('dve', 'memset', Wz, 0.0, writes=[Wz])
        for j in range(2):
            dma('sp', fm32[:, j, :], dr['fourier_mix'][2 * j:2 * j + 2].rearrange("g c d -> (g c) d"), 'fm%d' % j)
            OP('dve', 'tensor_copy', fmst[:, j, :], fm32[:, j, :], reads=[fm32[:, j, :]], writes=[fmst[:, j, :]])
            for t, bd in enumerate((bdcos, bdsin)):
                b = P.bank()
                mm(psf(b)[:, 0:64], bd, fmst[:, j, :], True, True)
                for g in range(2):
                    o_ = Wz[g * 64:(g + 1) * 64, j, t * 128 + g * 64: t * 128 + (g + 1) * 64]
                    i_ = psf(b)[g * 64:(g + 1) * 64, 0:64]
                    OP('dve', 'tensor_copy', o_, i_, reads=[i_], writes=[o_])

        hT = view(OFF['hT'], 8 * 2048, pat="p (c t) -> p c t", c=8)
        t0 = OFF['temps']
        xn = [view(t0 + i * 1024, 1024) for i in range(4)]
        misc = t0 + 3072
        yfT = view(OFF['yfT'], 2 * 2048, pat="p (c t) -> p c t", c=2)
        yaT = view(OFF['yaT'], 4 * 2048, pat="p (h t) -> p h t", h=4)
        X0 = OFF['X']
        U0 = OFF['U']
        xres = view(X0, 16 * 2048, F32, pat="p (n d) -> p n d", n=16)
        NSLOT = 5
        wslots = [view(OFF['wring'] + i * 4096, 4096) for i in range(NSLOT)]
        wstate = {'n': 0}

        def wpiece(src3, parts=128, a=8):
            i = wstate['n'] % NSLOT
            wstate['n'] += 1
            dst = wslots[i][0:parts, :].rearrange("p (a b) -> p a b", a=a)[:, :, 0:src3.shape[-1]]
            dma('pool', dst, src3, 'w%d' % i)
            return dst

        def wcols(name, c0_, ncol, nk=8):
            return dr[name][:, c0_:c0_ + ncol].rearrange("(k p) j -> p k j", p=128)

        stat_ctr = [0]

        def rms_stats(xt, junk):
            s_ = stat_ctr[0] % 32
            stat_ctr[0] += 1
            ss = ST[:, s_:s_ + 1]
            ms = ST[:, 64 + s_:64 + s_ + 1]
            rstd = ST[:, 128 + s_:128 + s_ + 1]
            OP('act', 'activation', out=junk, in_=xt, func=AF.Square, accum_out=ss, reads=[xt], writes=[junk, ss])
            OP('act', 'activation', out=ms, in_=ss, func=AF.Identity, scale=1.0 / D, bias=negh[:, 1:2], reads=[ss, negh[:, 1:2]], writes=[ms])
            OP('pool', 'tensor_tensor', rstd, ms, negh[:, 0:1], ALU.pow, reads=[ms, negh[:, 0:1]], writes=[rstd])
            return rstd

        nt_ctr = [0]

        def norm_A(xt):
            k = nt_ctr[0] % 4
            nt_ctr[0] += 1
            rstd = rms_stats(xt, xn[k])
            OP('dve', 'tensor_scalar_mul', xn[k], xt, rstd, reads=[xt, rstd], writes=[xn[k]])
            return k

        def norm_B(k, gi, dst):
            b = P.bank()
            pb = psb(b)
            for dc in range(8):
                o_ = pb[:, dc * 128:(dc + 1) * 128]
                i_ = xn[k][:, dc * 128:(dc + 1) * 128]
                OP('pe', 'transpose', o_, i_, ident, reads=[i_, ident], writes=[o_])
            gb = gT[:, gi, :].unsqueeze(2).to_broadcast([128, 8, 128])
            pin = pb.rearrange("p (c t) -> p c t", c=8)
            OP('dve', 'tensor_tensor', dst, pin, gb, ALU.mult, reads=[pin, gT[:, gi, :]], writes=[dst])

        def norm_many(items):
            prev = None
            for xt, gi, dst, pre in items:
                if pre is not None:
                    pre()
                k = norm_A(xt)
                if prev is not None:
                    norm_B(*prev)
                prev = (k, gi, dst)
            norm_B(*prev)

        pendB = []

        def norm_defer(items):
            assert not pendB
            for xt, gi, dst in items:
                pendB.append((norm_A(xt), gi, dst))

        def norm_pop(n=1):
            for _ in range(n):
                if pendB:
                    norm_B(*pendB.pop(0))

        def req_up(p_):
            return [wpiece(wcols('w_up', p_ * 1024 + i * 512, 512)) for i in range(2)]

        def req_dn(p_, i):
            return wpiece(dr['w_down'][p_ * 1024:(p_ + 1) * 1024, i * 512:(i + 1) * 512].rearrange("(k p) j -> p k j", p=128))

        prefetched = set()
        for b_ in range(nseq):
            xs = dr['x'][b_]
            xin = [view(X0 + i * 2048, 2048, F32) for i in range(4)]
            def mk_pre(t):
                if t < 4 and b_ in prefetched:
                    return None
                return lambda: dma('sp', xin[t % 4], xs[t * 128:(t + 1) * 128, :], 'xin%d' % (t % 4))
            memnT = view(OFF['memnT'], 2048, pat="p (c m) -> p c m", c=8)
            memst = [view(X0 + 8192 + i * 2048, 2048, F32) for i in range(2)]

            def mk_mpre(mt):
                return lambda: dma('sp', memst[mt], dr['mem'][b_][mt * 128:(mt + 1) * 128, :], 'mem%d' % mt)
            norm_many([(xin[t % 4], 0, hT[:, :, t * 128:(t + 1) * 128], mk_pre(t)) for t in range(NT)]
                      + [(memst[mt], 2, memnT[:, :, mt * 128:(mt + 1) * 128], mk_mpre(mt)) for mt in range(2)])

            if debug:
                dma('sp', dbg['d_hT'], hT, 'dbg')
            FB = X0 + 8192
            uT = view(FB, 4096, pat="p (c t) -> p c t", c=2)
            Z = view(FB + 4096, 8192, pat="p (n z) -> p n z", n=16)
            dring = [view(FB + 12288 + i * 1152, 1152) for i in range(4)]
            wu = wpiece(wcols('w_in', 0, 256))
            for tg in range(NTG):
                for j in range(2):
                    b = P.bank()
                    for dc in range(8):
                        mm(psf(b), wu[:, dc, j * 128:(j + 1) * 128], hT[:, dc, tg * 512:(tg + 1) * 512], dc == 0, dc == 7)
                    o_ = uT[:, j, tg * 512:(tg + 1) * 512]
                    OP('act', 'activation', out=o_, in_=psf(b), func=AF.Copy, reads=[psf(b)], writes=[o_])
            for t in range(NT):
                b = P.bank()
                for j in range(2):
                    mm(psf(b)[:, j * 256:(j + 1) * 256], uT[:, j, t * 128:(t + 1) * 128], Wz[:, j, :], True, True)
                OP('dve', 'tensor_copy', Z[:, t, :], psf(b), reads=[psf(b)], writes=[Z[:, t, :]])
            tmpB = [view(FB + 16896 + i * 1024, 1024, F32) for i in range(2)]
            bM = P.bank()
            for j in range(2):
                for sc in range(NT):
                    mm(psf(bM)[:, j:j + 1], Z[:, sc, j * 256: j * 256 + 128], altc, sc == 0, sc == NT - 1, sgc=True)
            for j in range(2):
                OP('act', 'activation', out=yfT[:, j, 1024:1025], in_=psf(bM)[:, j:j + 1], func=AF.Copy,
                   reads=[psf(bM)[:, j:j + 1]], writes=[yfT[:, j, 1024:1025]])
            bA = [[P.bank(), P.bank()] for _ in range(2)]
            bB = [[P.bank(), P.bank()] for _ in range(2)]
            dctr = 0
            for sc in range(NT):
                first = (sc == 0)
                last = (sc == NT - 1)
                tc_ = dring[dctr % 4]
                dma('sp', tc_[:, 0:1024], dr['dcos'][sc * 128:(sc + 1) * 128, 0:1024], 'dft%d' % (dctr % 4))
                dctr += 1
                for j in range(2):
                    lc = Z[:, sc, j * 256: j * 256 + 128]
                    mm(psf(bA[j][0]), lc, tc_[:, 0:512], first, last)
                    mm(psf(bA[j][1]), lc, tc_[:, 512:1024], first, last)
                ts_ = dring[dctr % 4]
                dma('sp', ts_[:, 0:1024], dr['dnsin'][sc * 128:(sc + 1) * 128, 0:1024], 'dft%d' % (dctr % 4))
                dctr += 1
                for j in range(2):
                    ls = Z[:, sc, j * 256 + 128: j * 256 + 256]
                    mm(psf(bB[j][0]), ls, ts_[:, 0:512], first, last)
                    mm(psf(bB[j][1]), ls, ts_[:, 512:1024], first, last)
            for j in range(2):
                for q_ in range(2):
                    tb = tmpB[q_]
                    OP('act', 'activation', out=tb, in_=psf(bB[j][q_]), func=AF.Copy, reads=[psf(bB[j][q_])], writes=[tb])
                    fwd = yfT[:, j, q_ * 512:(q_ + 1) * 512]
                    OP('dve', 'tensor_tensor', fwd, psf(bA[j][q_]), tb, ALU.add, reads=[psf(bA[j][q_]), tb], writes=[fwd])
                    i0 = 1 if q_ == 0 else 0
                    n_ = 512 - i0
                    st_ = 2048 - (q_ * 512 + i0)
                    dst = yfT[:, j, st_:st_ - n_:-1]
                    OP('dve', 'tensor_tensor', dst, psf(bA[j][q_])[:, i0:512], tb[:, i0:512], ALU.subtract,
                       reads=[psf(bA[j][q_])[:, i0:512], tb[:, i0:512]], writes=[dst])

            acc = view(X0, 16384, F32, pat="p (h t) -> p h t", h=4)
            qT = view(X0 + 16384, 4096, pat="p (c t) -> p c t", c=2)
            kT = view(X0 + 20480, 4096, pat="p (c t) -> p c t", c=2)
            vA = view(X0 + 24576, 5120, pat="p (n h e) -> p n h e", n=16, h=4)
            PT = [view(U0 + i * 1024, 1024, pat="p (h q) -> p h q", h=2) for i in range(3)]
            qb = [view(U0 + 3072 + i * 512, 512) for i in range(2)]
            rt = [view(U0 + 4096 + i * 1024, 1024, F32) for i in range(4)]
            ropec = view(U0 + 8192, 2048)
            ropes = view(U0 + 10240, 2048)
            dma('sp', ropec, dr['ropec'], 'rc')
            dma('sp', ropes, dr['ropes'], 'rs')
            rctr = [0]
            for g, dil in enumerate((1, 4, 16)):
                L = S // dil
                wq = wpiece(wcols('w_in', 256 + g * 256, 256), a=8)
                wk = wpiece(wcols('w_in', 1024 + g * 256, 256), a=8)
                wv = wpiece(wcols('w_in', 1792 + g * 256, 256), a=8)
                items = [(dstT, wsl, c, tg) for dstT, wsl in ((qT, wq), (kT, wk)) for c in range(2) for tg in range(NTG)]

                def qk_stage1(it):
                    dstT, wsl, c, tg = it
                    b = P.bank()
                    for dc in range(8):
                        mm(psf(b), wsl[:, dc, c * 128:(c + 1) * 128], hT[:, dc, tg * 512:(tg + 1) * 512], dc == 0, dc == 7)
                    k2 = rctr[0] % 2
                    rctr[0] += 1
                    OP('act', 'activation', out=qb[k2], in_=psf(b), func=AF.Copy, reads=[psf(b)], writes=[qb[k2]])
                    return (it, b, k2)

                def qk_stage2(st):
                    (dstT, wsl, c, tg), b, k2 = st
                    b2 = P.bank()
                    mm(psf(b2), perm, qb[k2], True, True)
                    t1 = rt[2 * k2]; t2 = rt[2 * k2 + 1]
                    OP('dve', 'tensor_tensor', t1, psf(b), ropec[:, tg * 512:(tg + 1) * 512], ALU.mult,
                       reads=[psf(b), ropec[:, tg * 512:(tg + 1) * 512]], writes=[t1])
                    OP('dve', 'tensor_tensor', t2, psf(b2), ropes[:, tg * 512:(tg + 1) * 512], ALU.mult,
                       reads=[psf(b2), ropes[:, tg * 512:(tg + 1) * 512]], writes=[t2])
                    n_j = 512 // dil
                    o_ = dstT[:, c, :].rearrange("p (r j) -> p r j", r=dil)[:, :, tg * n_j:(tg + 1) * n_j]
                    i1 = t1.rearrange("p (j r) -> p r j", r=dil)
                    i2 = t2.rearrange("p (j r) -> p r j", r=dil)
                    OP('pool', 'tensor_tensor', o_, i1, i2, ALU.add, reads=[t1, t2], writes=[o_])

                prev_st = None
                for it in items:
                    st_ = qk_stage1(it)
                    if prev_st is not None:
                        qk_stage2(prev_st)
                    prev_st = st_
                qk_stage2(prev_st)
                ones_col = vA[:, :, :, 64:65]
                OP('pool', 'memset', ones_col, 1.0, writes=[ones_col])
                ntl = L // 128
                for r in range(dil):
                    for jt in range(ntl):
                        n_ = r * ntl + jt
                        b = P.bank()
                        for dc in range(8):
                            lhsT = hT[:, dc, bass.ds(r + dil * 128 * jt, 128, step=dil)]
                            mm(psf(b)[:, 0:256], lhsT, wv[:, dc, :], dc == 0, dc == 7)
                        o_ = vA[:, n_, :, 0:64]
                        i_ = psf(b)[:, 0:256].rearrange("p (h e) -> p h e", h=4)
                        if n_ % 2 == 0:
                            OP('act', 'activation', out=o_, in_=i_, func=AF.Copy, reads=[i_], writes=[o_])
                        else:
                            OP('dve', 'tensor_copy', o_, i_, reads=[i_], writes=[o_])
                visits = []
                if ntl == 1:
                    for r4 in range(0, dil, 4):
                        for c in range(2):
                            visits.append(dict(r=r4, c=c, kt=0, lo=0, hi=0, mcol=128, ocol=None, first=True,
                                               fin=('g2', r4), job=('g2', r4, c)))
                else:
                    jobs = []
                    for r in range(dil):
                        for c in range(2):
                            for seg in range(ntl // 4):
                                b0 = seg * 4; b1 = b0 + 3
                                kts = list(range(max(0, b0 - 1), min(ntl - 1, b1 + 1) + 1))
                                jv = []
                                for kt in kts:
                                    lo = max(kt - 1, b0); hi = min(kt + 1, b1)
                                    jv.append(dict(r=r, c=c, kt=kt, lo=lo, hi=hi, mcol=(lo - (kt - 1)) * 128,
                                                   ocol=(lo - b0) * 128, first=(kt == kts[0]),
                                                   fin=(('seg', r, b0) if kt == kts[-1] else None), job=(r, c, seg)))
                                jobs.append(jv)
                    for a_ in range(0, len(jobs), 2):
                        ja = jobs[a_]
                        jb = jobs[a_ + 1] if a_ + 1 < len(jobs) else []
                        for i_ in range(max(len(ja), len(jb))):
                            if i_ < len(ja):
                                visits.append(ja[i_])
                            if i_ < len(jb):
                                visits.append(jb[i_])
                mflat = masks.rearrange("p m q -> p (m q)")
                obanks = None

                def emit_pv(v, obs, slot):
                    nq = (v['hi'] - v['lo'] + 1) * 128
                    if v['ocol'] is None:
                        for par in range(2):
                            hs = 2 * v['c'] + par
                            for j in range(4):
                                o_ = PSt[0:65, obs[par], j * 128:(j + 1) * 128]
                                lhsT = vA[:, v['r'] + j, hs, 0:65]
                                mm(o_, lhsT, PT[slot][:, par, j * 128:(j + 1) * 128], j == 0, j == 3, sgc=True)
                        for par in range(2):
                            hs = 2 * v['c'] + par
                            src = PSt[0:65, obs[par], :].rearrange("p (j q) -> p j q", j=4)
                            dst = acc[0:65, hs, :].rearrange("p (q r) -> p r q", r=dil)[:, v['r']:v['r'] + 4, :]
                            OP('dve', 'tensor_tensor', dst, src, dst, ALU.add, reads=[src, dst], writes=[dst])
                        return
                    for par in range(2):
                        hs = 2 * v['c'] + par
                        o_ = PSt[0:65, obs[par], v['ocol']:v['ocol'] + nq]
                        lhsT = vA[:, v['r'] * ntl + v['kt'], hs, 0:65]
                        mm(o_, lhsT, PT[slot][:, par, 0:nq], v['first'], v['fin'] is not None, sgc=True)
                    f = v['fin']
                    if f is None:
                        return
                    _, r_, b0_ = f
                    for par in range(2):
                        src = PSt[0:65, obs[par], :]
                        dst = acc[0:65, 2 * v['c'] + par, bass.ds(r_ + dil * 128 * b0_, 512, step=dil)]
                        if g == 0:
                            OP('dve', 'tensor_copy', dst, src, reads=[src], writes=[dst])
                        else:
                            OP('dve', 'tensor_tensor', dst, src, dst, ALU.add, reads=[src, dst], writes=[dst])

                LAG = 2
                pendq = []

                def flush(n_keep):
                    while len(pendq) > n_keep:
                        emit_pv(*pendq.pop(0))

                open_ob = set()
                job_ob = {}

                def abank_pair():
                    tries = 0
                    while True:
                        if P.nbank % 2 == 1:
                            P.bank()
                        b_ = P.nbank % 8
                        if b_ in open_ob:
                            P.bank(); P.bank()
                            continue
                        if any((b_ in p_[1]) for p_ in pendq) and tries < 4:
                            P.bank(); P.bank()
                            tries += 1
                            continue
                        break
                    if any((b_ in p_[1]) or ((b_ + 1) in p_[1]) for p_ in pendq):
                        flush(0)
                    P.bank(); P.bank()
                    return b_

                for vi, v in enumerate(visits):
                    c = v['c']
                    r = v['r']
                    nq = (v['hi'] - v['lo'] + 1) * 128
                    if v['first']:
                        ob_ = abank_pair()
                        job_ob[v['job']] = (ob_, ob_ + 1)
                        open_ob |= {ob_, ob_ + 1}
                    obanks = job_ob[v['job']]
                    sb = abank_pair()
                    slot = vi % 3
                    if v['ocol'] is None:
                        nq = 512
                        for par in range(2):
                            pb_ = par * 64
                            for j in range(4):
                                rr = r + j
                                lhsT = kT[pb_:pb_ + 64, c, rr * L: rr * L + 128]
                                rhs = qT[pb_:pb_ + 64, c, rr * L: rr * L + 128]
                                mm(psf(sb + par)[:, j * 128:(j + 1) * 128], lhsT, rhs, j == 0, j == 3, sgc=True)
                        mband = mflat[:, 128:256]
                        mk = mband.unsqueeze(1).unsqueeze(1).to_broadcast([128, 2, 4, 128])
                        s_in = PSt[:, sb:sb + 2, :]
                        p_out = PT[slot][:, :, 0:512]
                        OP('act', 'activation', out=p_out, in_=s_in, func=AF.Exp, scale=0.125, reads=[s_in], writes=[p_out])
                        p4 = p_out.rearrange("p h (j q) -> p h j q", j=4)
                        OP('dve', 'tensor_tensor', p4, p4, mk, ALU.mult, reads=[p_out, mband], writes=[p_out])
                    else:
                        for par in range(2):
                            pb_ = par * 64
                            lhsT = kT[pb_:pb_ + 64, c, r * L + v['kt'] * 128: r * L + (v['kt'] + 1) * 128]
                            rhs = qT[pb_:pb_ + 64, c, r * L + v['lo'] * 128: r * L + (v['hi'] + 1) * 128]
                            mm(psf(sb + par)[:, 0:nq], lhsT, rhs, True, True)
                        s_in = PSt[:, sb:sb + 2, 0:nq]
                        p_out = PT[slot][:, :, 0:nq]
                        OP('act', 'activation', out=p_out, in_=s_in, func=AF.Exp, scale=0.125, reads=[s_in], writes=[p_out])
                        mk = mflat[:, v['mcol']:v['mcol'] + nq].unsqueeze(1).to_broadcast([128, 2, nq])
                        OP('dve', 'tensor_tensor', p_out, p_out, mk, ALU.mult,
                           reads=[p_out, mflat[:, v['mcol']:v['mcol'] + nq]], writes=[p_out])
                    pendq.append((v, obanks, slot))
                    if v['fin'] is not None:
                        open_ob -= set(obanks)
                    flush(LAG)
                flush(0)
            if debug:
                dma('sp', dbg['d_acc'], acc, 'dbg')
                dma('sp', dbg['d_q'], qT, 'dbg')
                dma('sp', dbg['d_k'], kT, 'dbg')
                dma('sp', dbg['d_v'], vA, 'dbg')
            rden = view(U0, 1024, F32)
            for tg in range(NTG):
                for hs in range(4):
                    b = P.bank()
                    rhs = acc[0:65, hs, tg * 512:(tg + 1) * 512]
                    mm(PSt[0:64, b, :], sel[0:65, :], rhs, True, True)
                    OP('act', 'activation', out=rden[0:64, :], in_=PSt[0:64, b, :], func=AF.Ln, reads=[PSt[0:64, b, :]], writes=[rden[0:64, :]])
                    OP('act', 'activation', out=rden[0:64, :], in_=rden[0:64, :], func=AF.Exp, scale=-1.0, reads=[rden[0:64, :]], writes=[rden[0:64, :]])
                    o_ = yaT[0:64, hs, tg * 512:(tg + 1) * 512]
                    OP('dve', 'tensor_tensor', o_, acc[0:64, hs, tg * 512:(tg + 1) * 512], rden[0:64, :], ALU.mult,
                       reads=[acc[0:64, hs, tg * 512:(tg + 1) * 512], rden[0:64, :]], writes=[o_])

            if debug:
                dma('sp', dbg['d_yfT'], yfT, 'dbg')
                dma('sp', dbg['d_yaT'], yaT, 'dbg')
            mergedT = view(U0 + 1024, 4096, pat="p (c t) -> p c t", c=8)
            gt = [view(U0 + 5120 + i * 1024, 1024, F32) for i in range(4)]
            wfb_ = view(U0 + 9216, 2048, pat="p (k j) -> p k j", k=2)
            wab = view(U0 + 11264, 4096, pat="p (h j) -> p h j", h=4)
            dma('pool', wfb_, dr['w_fourier_branch'].rearrange("(k p) j -> p k j", p=128), 'wfa')
            dma('pool', wab[0:64], dr['w_attn_branch'].rearrange("(h e) j -> e h j", e=64), 'wab')
            def req_gates(which):
                out_ = []
                for nm_, i_ in which:
                    out_.append(wpiece(wcols('w_in', (2560 if nm_ == 'f' else 3584) + i_ * 512, 512)))
                return out_
            gcur = req_gates([('f', 0), ('a', 0), ('f', 1), ('a', 1)])
            for tg in range(NTG):
                tsl = slice(tg * 512, (tg + 1) * 512)
                wgf = [gcur[0], gcur[2]]; wga = [gcur[1], gcur[3]]
                for mc in range(8):
                    if mc == 4:
                        wmx = [wpiece(wcols('w_mix_out', i * 512, 512)) for i in range(2)]
                    b1 = P.bank()
                    for dc in range(8):
                        mm(psf(b1), wgf[mc // 4][:, dc, (mc % 4) * 128:(mc % 4 + 1) * 128], hT[:, dc, tsl], dc == 0, dc == 7)
                    b2 = P.bank()
                    for dc in range(8):
                        mm(psf(b2), wga[mc // 4][:, dc, (mc % 4) * 128:(mc % 4 + 1) * 128], hT[:, dc, tsl], dc == 0, dc == 7)
                    if mc in (1, 2, 3, 4):
                        norm_pop()
                    b3 = P.bank()
                    for j in range(2):
                        mm(psf(b3), wfb_[:, j, mc * 128:(mc + 1) * 128], yfT[:, j, tsl], j == 0, j == 1)
                    b4 = P.bank()
                    for hs in range(4):
                        mm(psf(b4), wab[0:64, hs, mc * 128:(mc + 1) * 128], yaT[0:64, hs, tsl], hs == 0, hs == 3)
                    OP('act', 'activation', out=gt[0], in_=psf(b1), func=AF.Tanh, scale=0.5, reads=[psf(b1)], writes=[gt[0]])
                    OP('act', 'activation', out=gt[1], in_=psf(b2), func=AF.Tanh, scale=0.5, reads=[psf(b2)], writes=[gt[1]])
                    OP('dve', 'scalar_tensor_tensor', gt[2], gt[0], 1.0, psf(b3), ALU.add, ALU.mult,
                       reads=[gt[0], psf(b3)], writes=[gt[2]])
                    OP('dve', 'scalar_tensor_tensor', gt[3], gt[1], 1.0, psf(b4), ALU.add, ALU.mult,
                       reads=[gt[1], psf(b4)], writes=[gt[3]])
                    OP('pool', 'tensor_tensor', mergedT[:, mc, :], gt[2], gt[3], ALU.add,
                       reads=[gt[2], gt[3]], writes=[mergedT[:, mc, :]])
                if tg + 1 < NTG:
                    gnext = req_gates([('f', 0), ('a', 0), ('f', 1)])
                else:
                    wkp_pre = [wpiece(wcols('w_ckv', half * 512, 512)) for half in range(2)]
                    wvp_pre = [wpiece(wcols('w_ckv', 1024, 512))]
                for tt in range(4):
                    t = tg * 4 + tt
                    dma('sp', xres[:, t, :], xs[t * 128:(t + 1) * 128, :], 'xr%d' % (t % 4))
                    for nh in range(2):
                        b = P.bank()
                        for fc in range(8):
                            mm(psf(b), mergedT[:, fc, tt * 128:(tt + 1) * 128], wmx[nh][:, fc, :], fc == 0, fc == 7)
                        o_ = xres[:, t, nh * 512:(nh + 1) * 512]
                        OP('dve', 'scalar_tensor_tensor', o_, psf(b), 0.5, o_, ALU.mult, ALU.add,
                           reads=[psf(b), o_], writes=[o_])
                if tg + 1 < NTG:
                    gnext += req_gates([('a', 1)])
                    gcur = gnext
                else:
                    wvp_pre.append(wpiece(wcols('w_ckv', 1024 + 512, 512)))
                norm_defer([(xres[:, tg * 4 + tt, :], 1, hT[:, :, (tg * 4 + tt) * 128:(tg * 4 + tt + 1) * 128]) for tt in range(4)])

            if debug:
                dma('sp', dbg['d_x1'], xres, 'dbg')
            KT = view(U0, 2048, pat="p (c m) -> p c m", c=8)
            Vm = view(U0 + 2048, 2048, pat="p (m f) -> p m f", m=2)
            qcT = view(U0 + 4096, 4096, pat="p (c t) -> p c t", c=8)
            PTc = [view(U0 + 8192 + i * 1024, 1024, pat="p (m t) -> p m t", m=2) for i in range(2)]
            rdc = view(U0 + 10240, 1024, F32)
            ocT = view(U0 + 11264, 4096, pat="p (c t) -> p c t", c=8)
            for half in range(2):
                wkp = wkp_pre[half]
                for fc4 in range(4):
                    fc = half * 4 + fc4
                    b = P.bank()
                    for dc in range(8):
                        mm(psf(b)[:, 0:256], wkp[:, dc, fc4 * 128:(fc4 + 1) * 128], memnT[:, dc, :], dc == 0, dc == 7)
                    OP('act', 'activation', out=KT[:, fc, :], in_=psf(b)[:, 0:256], func=AF.Copy,
                       reads=[psf(b)[:, 0:256]], writes=[KT[:, fc, :]])
            for half in range(2):
                wvp = wvp_pre[half]
                for mt in range(2):
                    b = P.bank()
                    for dc in range(8):
                        mm(psf(b), memnT[:, dc, mt * 128:(mt + 1) * 128], wvp[:, dc, :], dc == 0, dc == 7)
                    o_ = Vm[:, mt, half * 512:(half + 1) * 512]
                    OP('dve', 'tensor_copy', o_, psf(b), reads=[psf(b)], writes=[o_])
            norm_pop(4)
            wcq = [wpiece(wcols('w_cq', i * 512, 512)) for i in range(2)]
            wco = [wpiece(wcols('w_co', i * 512, 512)) for i in range(2)]
            wup_pre = [wpiece(wcols('w_up', 0, 512))]
            wdn_pre = []
            for tg in range(NTG):
                tsl = slice(tg * 512, (tg + 1) * 512)
                for fc in range(8):
                    b = P.bank()
                    for dc in range(8):
                        mm(psf(b), wcq[fc // 4][:, dc, (fc % 4) * 128:(fc % 4 + 1) * 128], hT[:, dc, tsl], dc == 0, dc == 7)
                    if fc in (1, 2, 3, 4):
                        norm_pop()
                    if fc % 2 == 0:
                        OP('act', 'activation', out=qcT[:, fc, :], in_=psf(b), func=AF.Copy, reads=[psf(b)], writes=[qcT[:, fc, :]])
                    else:
                        OP('dve', 'tensor_copy', qcT[:, fc, :], psf(b), reads=[psf(b)], writes=[qcT[:, fc, :]])
                if tg == NTG - 1:
                    wup_pre.append(wpiece(wcols('w_up', 512, 512)))
                    wdn_pre.append(req_dn(0, 0))
                for h in range(4):
                    pt = PTc[h % 2]
                    for mt in range(2):
                        b = P.bank()
                        for ec in range(2):
                            mm(psf(b), KT[:, 2 * h + ec, mt * 128:(mt + 1) * 128], qcT[:, 2 * h + ec, :], ec == 0, ec == 1)
                        OP('act', 'activation', out=pt[:, mt, :], in_=psf(b), func=AF.Exp, scale=1.0 / 16.0,
                           reads=[psf(b)], writes=[pt[:, mt, :]])
                    bd_ = P.bank()
                    for mt in range(2):
                        mm(psf(bd_), ones, pt[:, mt, :], mt == 0, mt == 1)
                    OP('act', 'activation', out=rdc, in_=psf(bd_), func=AF.Ln, reads=[psf(bd_)], writes=[rdc])
                    OP('act', 'activation', out=rdc, in_=rdc, func=AF.Exp, scale=-1.0, reads=[rdc], writes=[rdc])
                    for ec in range(2):
                        b = P.bank()
                        for mt in range(2):
                            mm(psf(b), Vm[:, mt, h * 256 + ec * 128: h * 256 + (ec + 1) * 128], pt[:, mt, :], mt == 0, mt == 1)
                        o_ = ocT[:, 2 * h + ec, :]
                        OP('dve', 'tensor_tensor', o_, psf(b), rdc, ALU.mult, reads=[psf(b), rdc], writes=[o_])
                for tt in range(4):
                    t = tg * 4 + tt
                    for nh in range(2):
                        b = P.bank()
                        for fc in range(8):
                            mm(psf(b), ocT[:, fc, tt * 128:(tt + 1) * 128], wco[nh][:, fc, :], fc == 0, fc == 7)
                        o_ = xres[:, t, nh * 512:(nh + 1) * 512]
                        OP('dve', 'tensor_tensor', o_, psf(b), o_, ALU.add, reads=[psf(b), o_], writes=[o_])
                norm_defer([(xres[:, tg * 4 + tt, :], 3, hT[:, :, (tg * 4 + tt) * 128:(tg * 4 + tt + 1) * 128]) for tt in range(4)])
            if debug:
                dma('sp', dbg['d_x2'], xres, 'dbg')

            a2T = [view(U0 + i * 4096, 4096, pat="p (c t) -> p c t", c=8) for i in range(2)]
            rl = [view(U0 + 8192 + i * 512, 512) for i in range(2)]
            ostage = [view(U0 + 9216 + i * 2048, 2048, F32) for i in range(2)]
            gfin = view(U0 + 13312, 2048, F32)
            OP('sp', 'dma_start', out=gfin, in_=dr['norm_final'].partition_broadcast(128),
               reads=[], writes=[gfin], dma_key='gfin')
            rlc = 0
            def up(tg, wup):
                tsl = slice(tg * 512, (tg + 1) * 512)
                for fc in range(8):
                    b = P.bank()
                    for dc in range(8):
                        mm(psf(b), wup[fc // 4][:, dc, (fc % 4) * 128:(fc % 4 + 1) * 128], hT[:, dc, tsl], dc == 0, dc == 7)
                    r_ = rl[rlc[0] % 2]
                    rlc[0] += 1
                    OP('act', 'activation', out=r_, in_=psf(b), func=AF.Relu, reads=[psf(b)], writes=[r_])
                    o_ = a2T[tg % 2][:, fc, :]
                    OP('pool', 'tensor_tensor', o_, r_, r_, ALU.mult, reads=[r_], writes=[o_])

            def down(tg, wdn, last):
                for tt in range(4):
                    t = tg * 4 + tt
                    for nh in range(2):
                        b = P.bank()
                        for fc in range(8):
                            mm(psf(b), a2T[tg % 2][:, fc, tt * 128:(tt + 1) * 128], wdn[nh][:, fc, :], fc == 0, fc == 7)
                        o_ = xres[:, t, nh * 512:(nh + 1) * 512]
                        OP('dve', 'tensor_tensor', o_, psf(b), o_, ALU.add, reads=[psf(b), o_], writes=[o_])
                    if last:
                        xt = xres[:, t, :]
                        kk_ = nt_ctr[0] % 4
                        nt_ctr[0] += 1
                        rstd = rms_stats(xt, xn[kk_])
                        os_ = ostage[t % 2]
                        OP('dve', 'scalar_tensor_tensor', os_, xt, rstd, gfin, ALU.mult, ALU.mult,
                           reads=[xt, rstd, gfin], writes=[os_])
                        dma('sp', out_d[b_][t * 128:(t + 1) * 128, :], os_, 'out%d' % (t % 2))

            rlc = [0]
            wup = wup_pre
            wdn = [wdn_pre[0], req_dn(0, 1)]
            for p_ in range(4):
                nxt_up = None; nxt_dn = None
                up(0, wup)
                norm_pop(4)
                for tg in range(NTG):
                    if tg + 1 < NTG:
                        up(tg + 1, wup)
                    if tg == NTG - 2 and p_ < 3:
                        nxt_up = req_up(p_ + 1)
                        nxt_dn = [req_dn(p_ + 1, 0)]
                    down(tg, wdn, p_ == 3)
                    if p_ == 3 and tg == 1 and b_ + 1 < nseq:
                        for t_ in range(4):
                            dma('sp', xin[t_], dr['x'][b_ + 1][t_ * 128:(t_ + 1) * 128, :], 'xin%d' % t_)
                        prefetched.add(b_ + 1)
                if p_ < 3:
                    nxt_dn.append(req_dn(p_ + 1, 1))
                    wup, wdn = nxt_up, nxt_dn

        P.emit(sem_alloc)
    return nc


_CACHE = {}


def kernel(**inputs):
    nseq = SEQ_PER_CORE
    consts = host_constants()
    x = np.ascontiguousarray(inputs['x'], dtype=np.float32)
    mem = np.ascontiguousarray(inputs['mem'], dtype=np.float32)
    B = x.shape[0]
    n_cores = B // nseq
    if 'nc' not in _CACHE:
        _CACHE['nc'] = build_program(nseq)
    nc = _CACHE['nc']
    shared = {}
    for name, shp in W_SPECS:
        a = np.asarray(inputs[name], dtype=np.float32)
        shared[name] = np.ascontiguousarray(a.reshape(shp))
    for name, shp, dt in CONST_SPECS:
        shared[name] = consts[name]
    in_maps = []
    for c in range(n_cores):
        m = dict(shared)
        m['x'] = x[c * nseq:(c + 1) * nseq]
        m['mem'] = mem[c * nseq:(c + 1) * nseq]
        in_maps.append(m)
    res = run_bass_kernel_spmd(nc, in_maps, core_ids=list(range(n_cores)))
    out = np.concatenate([np.asarray(r['out']) for r in res.results], axis=0)
    return out.astype(np.float32)
```

```python
import numpy as np
import ml_dtypes
import concourse.bass as bass
import concourse.mybir as mybir
from concourse.bass_utils import run_bass_kernel_spmd

F32 = mybir.dt.float32
BF16 = mybir.dt.bfloat16
AF = mybir.ActivationFunctionType
ALU = mybir.AluOpType
DSZ = {F32: 4, BF16: 2}

S = 2048
D = 1024
NT = 16
NTG = 4
NEG = -30000.0
N_CORES = 8
SEQ_PER_CORE = 2


class Prog:
    BLK = 256

    def __init__(self, nc):
        self.nc = nc
        self.ops = []
        self.lastw = {}
        self.readers = {}
        self.dma_prev = {}
        self.nbank = 0

    def res(self, ap):
        sp = str(ap.space)
        if 'SB' in sp:
            kind = 'sb'
        elif 'PS' in sp.upper():
            kind = 'ps'
        else:
            return []
        pat = ap.ap
        pstep = pat[0][0]
        free_off = ap.offset - ap.base_partition() * pstep
        sz = DSZ[ap.dtype]
        dims = [(s, c) for (s, c) in pat[1:]]
        if not dims:
            dims = [(1, 1)]
        inner_s, inner_c = dims[-1]
        outer = dims[:-1]
        nouter = 1
        for s, c in outer:
            nouter *= c
        ivals = []
        if nouter <= 512:
            starts = [0]
            for s, c in outer:
                starts = [b + i * s for b in starts for i in range(c)]
            for st in starts:
                lo = free_off + st
                if inner_s < 0:
                    lo -= (inner_c - 1) * (-inner_s)
                hi = lo + (inner_c - 1) * abs(inner_s) + 1
                ivals.append((lo * sz, hi * sz))
        else:
            lo = free_off
            hi = free_off + sum((c - 1) * abs(s) for s, c in dims) + 1
            ivals.append((lo * sz, hi * sz))
        out = set()
        if kind == 'ps':
            for lo, hi in ivals:
                for b in range(lo // 2048, (hi - 1) // 2048 + 1):
                    out.add(('ps', b))
        else:
            for lo, hi in ivals:
                for b in range(lo // self.BLK, (hi - 1) // self.BLK + 1):
                    out.add(b)
        return out

    def op(self, eng, name, *args, reads=(), writes=(), dma_key=None, **kw):
        i = len(self.ops)
        rset, wset = set(), set()
        for a in reads:
            r = self.res(a)
            for k in r:
                if isinstance(k, tuple):
                    wset.add(k)
                else:
                    rset.add(k)
        for a in writes:
            wset |= set(self.res(a))
        raw, oth = set(), set()
        for k in rset:
            j = self.lastw.get(k)
            if j is not None:
                raw.add(j)
        for k in wset:
            j = self.lastw.get(k)
            if j is not None:
                (raw if isinstance(k, tuple) else oth).add(j)
            for j in self.readers.get(k, ()):
                oth.add(j)
        if dma_key is not None:
            j = self.dma_prev.get(dma_key)
            if j is not None:
                raw.add(j)
            self.dma_prev[dma_key] = i
        for k in rset:
            self.readers.setdefault(k, []).append(i)
        for k in wset:
            self.lastw[k] = i
            self.readers[k] = []
        deps = set()
        for j in raw | oth:
            oj = self.ops[j]
            if oj['dma'] is None and oj['eng'] == eng:
                if eng == 'pe':
                    continue
            deps.add(j)
        self.ops.append(dict(eng=eng, name=name, args=args, kw=kw, deps=deps, dma=dma_key))
        return i

    def bank(self):
        b = self.nbank % 8
        self.nbank += 1
        return b

    def emit(self, sem_alloc):
        nc = self.nc
        engs = {'pe': nc.tensor, 'act': nc.scalar, 'dve': nc.vector, 'pool': nc.gpsimd, 'sp': nc.sync}
        n = len(self.ops)
        need = [False] * n
        for o in self.ops:
            for j in o['deps']:
                need[j] = True
        esem = {e: sem_alloc('e_' + e) for e in ('pe', 'act', 'dve', 'pool')}
        dsem = {}
        ecnt = {e: 0 for e in esem}
        dcnt = {}
        sig = [None] * n
        for i, o in enumerate(self.ops):
            if o['dma'] is not None:
                k = o['dma']
                if k not in dsem:
                    dsem[k] = sem_alloc('d_' + k)
                    dcnt[k] = 0
                dcnt[k] += 16
                sig[i] = (dsem[k], dcnt[k])
            elif need[i]:
                e = o['eng']
                ecnt[e] += 1
                sig[i] = (esem[e], ecnt[e])
        seen = {e: {} for e in engs}
        for i, o in enumerate(self.ops):
            e = o['eng']
            eo = engs[e]
            want = {}
            for j in o['deps']:
                sm, v = sig[j]
                if want.get(sm.num, (None, 0))[1] < v:
                    want[sm.num] = (sm, v)
            for num, (sm, v) in sorted(want.items()):
                if seen[e].get(num, 0) >= v:
                    continue
                eo.wait_ge(sm, v)
                seen[e][num] = v
            ins = getattr(eo, o['name'])(*o['args'], **o['kw'])
            if sig[i] is not None:
                ins.then_inc(sig[i][0], 16 if o['dma'] is not None else 1)
        for k, sm in dsem.items():
            if seen['sp'].get(sm.num, 0) < dcnt[k]:
                nc.sync.wait_ge(sm, dcnt[k])


def host_constants():
    bf = ml_dtypes.bfloat16
    c = {}
    c['ident'] = np.eye(128, dtype=np.float32).astype(bf)
    c['ones'] = np.ones((128, 128), np.float32).astype(bf)
    perm = np.zeros((128, 128), np.float32)
    for p in range(128):
        e = p % 64
        if e < 8:
            perm[p + 8, p] = 1.0
        elif e < 16:
            perm[p - 8, p] = 1.0
    c['perm'] = perm.astype(bf)
    kp = np.arange(128)[:, None]
    q = np.arange(128)[None, :]
    masks = np.stack([
        np.where(q - kp >= 64, 1.0, 0.0),
        np.where(np.abs(kp - q) <= 64, 1.0, 0.0),
        np.where(kp - q >= 64, 1.0, 0.0),
    ]).astype(np.float32)
    c['masks'] = masks.astype(bf)
    i64 = np.arange(64)
    ang = 2 * np.pi * np.outer(i64, i64) / 64.0
    bdc = np.zeros((128, 128)); bds = np.zeros((128, 128))
    for g in range(2):
        bdc[g * 64:(g + 1) * 64, g * 64:(g + 1) * 64] = np.cos(ang)
        bds[g * 64:(g + 1) * 64, g * 64:(g + 1) * 64] = np.sin(ang)
    c['bdcos'] = bdc.astype(np.float32).astype(bf)
    c['bdsin'] = bds.astype(np.float32).astype(bf)
    s = np.arange(S, dtype=np.int64)
    ph = (np.outer(s, s) % S).astype(np.float64) * (2 * np.pi / S)
    nrm = 1.0 / np.sqrt(S * 64.0)
    c['dcos'] = (np.cos(ph) * nrm).astype(np.float32).astype(bf)
    c['dnsin'] = (-np.sin(ph) * nrm).astype(np.float32).astype(bf)
    half = 8
    inv_freq = (np.float32(500000.0) ** (-np.arange(half, dtype=np.float32) / np.float32(half))).astype(np.float32)
    angr = (np.arange(S, dtype=np.float32)[:, None] * inv_freq[None, :]).astype(np.float32)
    cosr = np.cos(angr).astype(np.float32).T
    sinr = np.sin(angr).astype(np.float32).T
    rc = np.ones((128, S), np.float32)
    rs = np.zeros((128, S), np.float32)
    for p in range(128):
        e = p % 64
        if e < 8:
            rc[p] = cosr[e]; rs[p] = -sinr[e]
        elif e < 16:
            rc[p] = cosr[e - 8]; rs[p] = sinr[e - 8]
    c['ropec'] = rc.astype(bf)
    c['ropes'] = rs.astype(bf)
    c['alt'] = (((-1.0) ** np.arange(128))[:, None] * nrm).astype(np.float32).astype(bf)
    sel = np.zeros((128, 64), np.float32)
    sel[64, :] = 1.0
    c['sel'] = sel
    return c


CONST_SPECS = [
    ('ident', [128, 128], BF16), ('ones', [128, 128], BF16), ('perm', [128, 128], BF16),
    ('masks', [3, 128, 128], BF16), ('bdcos', [128, 128], BF16), ('bdsin', [128, 128], BF16),
    ('dcos', [S, S], BF16), ('dnsin', [S, S], BF16),
    ('ropec', [128, S], BF16), ('ropes', [128, S], BF16), ('sel', [128, 64], F32), ('alt', [128, 1], BF16),
]
W_SPECS = [
    ('norm_mix', [D]), ('w_in', [D, 4608]), ('fourier_mix', [4, 64, 64]), ('w_fourier_branch', [256, D]),
    ('w_attn_branch', [256, D]), ('w_mix_out', [D, D]), ('norm_cross', [D]), ('norm_mem', [D]),
    ('w_cq', [D, D]), ('w_ckv', [D, 2 * D]), ('w_co', [D, D]), ('norm_mlp', [D]),
    ('w_up', [D, 4 * D]), ('w_down', [4 * D, D]), ('norm_final', [D]),
]


def build_program(nseq, debug=False):
    nc = bass.Bass("TRN2", target_bir_lowering=False)
    dr = {}
    dr['x'] = nc.dram_tensor("x", [nseq, S, D], F32, kind="ExternalInput").ap()
    dr['mem'] = nc.dram_tensor("mem", [nseq, 256, D], F32, kind="ExternalInput").ap()
    for name, shp in W_SPECS:
        dr[name] = nc.dram_tensor(name, shp, F32, kind="ExternalInput").ap()
    for name, shp, dt in CONST_SPECS:
        dr[name] = nc.dram_tensor(name, shp, dt, kind="ExternalInput").ap()
    out_d = nc.dram_tensor("out", [nseq, S, D], F32, kind="ExternalOutput").ap()
    dbg = {}
    if debug:
        for nm, shp, dt in (('d_hT', [128, 8, 2048], BF16), ('d_yfT', [128, 2, 2048], BF16), ('d_yaT', [128, 4, 2048], BF16),
                            ('d_acc', [128, 4, 2048], F32), ('d_x1', [128, 16, 1024], F32), ('d_x2', [128, 16, 1024], F32),
                            ('d_q', [128, 2, 2048], BF16), ('d_k', [128, 2, 2048], BF16), ('d_v', [128, 16, 4, 80], BF16)):
            dbg[nm] = nc.dram_tensor(nm, shp, dt, kind="ExternalOutput").ap()

    OFF = {}
    cur = [0]

    def alloc(name, n):
        n = (n + 127) // 128 * 128
        OFF[name] = cur[0]
        cur[0] += n
        return OFF[name]

    alloc('const', 2816)
    alloc('memnT', 2048)
    alloc('wring', 5 * 4096)
    alloc('hT', 8 * 2048)
    alloc('temps', 4096)
    alloc('yfT', 2 * 2048)
    alloc('yaT', 4 * 2048)
    alloc('X', 16 * 1024 * 2)
    alloc('U', 15360)
    TOTAL = cur[0]

    sems = []
    import contextlib
    with contextlib.ExitStack() as es:
        A = es.enter_context(nc.sbuf_tensor("arena", [128, TOTAL], BF16))
        PSt = es.enter_context(nc.psum_tensor("ps", [128, 8, 512], F32))

        def sem_alloc(name):
            s_ = es.enter_context(nc.semaphore(name))
            sems.append(s_)
            return s_

        P = Prog(nc)

        def view(off, cnt_, dt=BF16, pat=None, **kw):
            v = A[:, off:off + cnt_]
            if dt == F32:
                v = v.bitcast(F32)
            if pat:
                v = v.rearrange(pat, **kw)
            return v

        def psf(b):
            return PSt[:, b, :]

        def psb(b):
            return PSt[:, b, :].bitcast(BF16)

        OP = P.op

        def mm(out, lhsT, rhs, start, stop, sgc=False):
            if sgc:
                OP('pe', 'matmul', out, lhsT, rhs, start=start, stop=stop, skip_group_check=True, reads=[lhsT, rhs], writes=[out])
            else:
                OP('pe', 'matmul', out, lhsT, rhs, start=start, stop=stop, reads=[lhsT, rhs], writes=[out])

        def dma(q, out, in_, key):
            OP(q, 'dma_start', out=out, in_=in_, reads=[in_], writes=[out], dma_key=key)

        c0 = OFF['const']
        ident = view(c0 + 0, 128)
        ones = view(c0 + 128, 128)
        perm = view(c0 + 256, 128)
        masks = view(c0 + 384, 384, pat="p (m q) -> p m q", m=3)
        bdcos = view(c0 + 768, 128)
        bdsin = view(c0 + 896, 128)
        Wz = view(c0 + 1024, 512, pat="p (j c) -> p j c", j=2)
        sel = view(c0 + 1536, 128, F32)
        gT = view(c0 + 1664, 128, F32, pat="p (n c) -> p n c", n=8)
        negh = view(c0 + 1792, 32, F32)
        ST = view(c0 + 1920, 384, F32)
        fmst = view(c0 + 2304, 128, pat="p (j d) -> p j d", j=2)
        fm32 = view(c0 + 2432, 256, F32, pat="p (j d) -> p j d", j=2)

        dma('sp', ident, dr['ident'], 'c')
        dma('sp', ones, dr['ones'], 'c1')
        dma('sp', perm, dr['perm'], 'c2')
        dma('sp', masks, dr['masks'].rearrange("m p q -> p m q"), 'c3')
        dma('sp', bdcos, dr['bdcos'], 'c4')
        dma('sp', bdsin, dr['bdsin'], 'c5')
        dma('sp', sel, dr['sel'], 'c6')
        altc = view(c0 + 2688, 2)[:, 0:1]
        dma('sp', altc, dr['alt'], 'c7')
        GIDX = {'norm_mix': 0, 'norm_cross': 1, 'norm_mem': 2, 'norm_mlp': 3}
        for nm, gi in GIDX.items():
            OP('pool', 'dma_start', out=gT[:, gi, :], in_=dr[nm].rearrange("(c p) -> p c", p=128),
               reads=[], writes=[gT[:, gi, :]], dma_key='g%d' % gi, allow_slow_non_contiguous=True)
        OP('dve', 'memset', negh, -0.5, writes=[negh])
        OP('dve', 'memset', negh[:, 1:2], 1e-6, writes=[negh[:, 1:2]])
        OP('dve', 'memset', Wz, 0.0, writes=[Wz])
        for j in range(2):
            dma('sp', fm32[:, j, :], dr['fourier_mix'][2 * j:2 * j + 2].rearrange("g c d -> (g c) d"), 'fm%d' % j)
            OP('dve', 'tensor_copy', fmst[:, j, :], fm32[:, j, :], reads=[fm32[:, j, :]], writes=[fmst[:, j, :]])
            for t, bd in enumerate((bdcos, bdsin)):
                b = P.bank()
                mm(psf(b)[:, 0:64], bd, fmst[:, j, :], True, True)
                for g in range(2):
                    o_ = Wz[g * 64:(g + 1) * 64, j, t * 128 + g * 64: t * 128 + (g + 1) * 64]
                    i_ = psf(b)[g * 64:(g + 1) * 64, 0:64]
                    OP('dve', 'tensor_copy', o_, i_, reads=[i_], writes=[o_])

        hT = view(OFF['hT'], 8 * 2048, pat="p (c t) -> p c t", c=8)
        t0 = OFF['temps']
        xn = [view(t0 + i * 1024, 1024) for i in range(4)]
        misc = t0 + 3072
        yfT = view(OFF['yfT'], 2 * 2048, pat="p (c t) -> p c t", c=2)
        yaT = view(OFF['yaT'], 4 * 2048, pat="p (h t) -> p h t", h=4)
        X0 = OFF['X']
        U0 = OFF['U']
        xres = view(X0, 16 * 2048, F32, pat="p (n d) -> p n d", n=16)
        NSLOT = 5
        wslots = [view(OFF['wring'] + i * 4096, 4096) for i in range(NSLOT)]
        wstate = {'n': 0}

        def wpiece(src3, parts=128, a=8):
            i = wstate['n'] % NSLOT
            wstate['n'] += 1
            dst = wslots[i][0:parts, :].rearrange("p (a b) -> p a b", a=a)[:, :, 0:src3.shape[-1]]
            dma('pool', dst, src3, 'w%d' % i)
            return dst

        def wcols(name, c0_, ncol, nk=8):
            return dr[name][:, c0_:c0_ + ncol].rearrange("(k p) j -> p k j", p=128)

        stat_ctr = [0]

        def rms_stats(xt, junk):
            s_ = stat_ctr[0] % 32
            stat_ctr[0] += 1
            ss = ST[:, s_:s_ + 1]
            ms = ST[:, 64 + s_:64 + s_ + 1]
            rstd = ST[:, 128 + s_:128 + s_ + 1]
            OP('act', 'activation', out=junk, in_=xt, func=AF.Square, accum_out=ss, reads=[xt], writes=[junk, ss])
            OP('act', 'activation', out=ms, in_=ss, func=AF.Identity, scale=1.0 / D, bias=negh[:, 1:2], reads=[ss, negh[:, 1:2]], writes=[ms])
            OP('pool', 'tensor_tensor', rstd, ms, negh[:, 0:1], ALU.pow, reads=[ms, negh[:, 0:1]], writes=[rstd])
            return rstd

        nt_ctr = [0]

        def norm_A(xt):
            k = nt_ctr[0] % 4
            nt_ctr[0] += 1
            rstd = rms_stats(xt, xn[k])
            OP('dve', 'tensor_scalar_mul', xn[k], xt, rstd, reads=[xt, rstd], writes=[xn[k]])
            return k

        def norm_B(k, gi, dst):
            b = P.bank()
            pb = psb(b)
            for dc in range(8):
                o_ = pb[:, dc * 128:(dc + 1) * 128]
                i_ = xn[k][:, dc * 128:(dc + 1) * 128]
                OP('pe', 'transpose', o_, i_, ident, reads=[i_, ident], writes=[o_])
            gb = gT[:, gi, :].unsqueeze(2).to_broadcast([128, 8, 128])
            pin = pb.rearrange("p (c t) -> p c t", c=8)
            OP('dve', 'tensor_tensor', dst, pin, gb, ALU.mult, reads=[pin, gT[:, gi, :]], writes=[dst])

        def norm_many(items):
            prev = None
            for xt, gi, dst, pre in items:
                if pre is not None:
                    pre()
                k = norm_A(xt)
                if prev is not None:
                    norm_B(*prev)
                prev = (k, gi, dst)
            norm_B(*prev)

        pendB = []

        def norm_defer(items):
            assert not pendB
            for xt, gi, dst in items:
                pendB.append((norm_A(xt), gi, dst))

        def norm_pop(n=1):
            for _ in range(n):
                if pendB:
                    norm_B(*pendB.pop(0))

        def req_up(p_):
            return [wpiece(wcols('w_up', p_ * 1024 + i * 512, 512)) for i in range(2)]

        def req_dn(p_, i):
            return wpiece(dr['w_down'][p_ * 1024:(p_ + 1) * 1024, i * 512:(i + 1) * 512].rearrange("(k p) j -> p k j", p=128))

        prefetched = set()
        for b_ in range(nseq):
            xs = dr['x'][b_]
            xin = [view(X0 + i * 2048, 2048, F32) for i in range(4)]
            def mk_pre(t):
                if t < 4 and b_ in prefetched:
                    return None
                return lambda: dma('sp', xin[t % 4], xs[t * 128:(t + 1) * 128, :], 'xin%d' % (t % 4))
            memnT = view(OFF['memnT'], 2048, pat="p (c m) -> p c m", c=8)
            memst = [view(X0 + 8192 + i * 2048, 2048, F32) for i in range(2)]

            def mk_mpre(mt):
                return lambda: dma('sp', memst[mt], dr['mem'][b_][mt * 128:(mt + 1) * 128, :], 'mem%d' % mt)
            norm_many([(xin[t % 4], 0, hT[:, :, t * 128:(t + 1) * 128], mk_pre(t)) for t in range(NT)]
                      + [(memst[mt], 2, memnT[:, :, mt * 128:(mt + 1) * 128], mk_mpre(mt)) for mt in range(2)])

            if debug:
                dma('sp', dbg['d_hT'], hT, 'dbg')
            FB = X0 + 8192
            uT = view(FB, 4096, pat="p (c t) -> p c t", c=2)
            Z = view(FB + 4096, 8192, pat="p (n z) -> p n z", n=16)
            NDR = 6
            dring = [view(FB + 12288 + i * 1152, 1152) for i in range(NDR)]
            wu = wpiece(wcols('w_in', 0, 256))
            for tg in range(NTG):
                for j in range(2):
                    b = P.bank()
                    for dc in range(8):
                        mm(psf(b), wu[:, dc, j * 128:(j + 1) * 128], hT[:, dc, tg * 512:(tg + 1) * 512], dc == 0, dc == 7)
                    o_ = uT[:, j, tg * 512:(tg + 1) * 512]
                    OP('act', 'activation', out=o_, in_=psf(b), func=AF.Copy, reads=[psf(b)], writes=[o_])
            for t in range(NT):
                b = P.bank()
                for j in range(2):
                    mm(psf(b)[:, j * 256:(j + 1) * 256], uT[:, j, t * 128:(t + 1) * 128], Wz[:, j, :], True, True)
                if t % 2 == 0:
                    OP('dve', 'tensor_copy', Z[:, t, :], psf(b), reads=[psf(b)], writes=[Z[:, t, :]])
                else:
                    OP('act', 'activation', out=Z[:, t, :], in_=psf(b), func=AF.Copy, reads=[psf(b)], writes=[Z[:, t, :]])
            tmpB = [view(FB + 12288 + NDR * 1152 + i * 1024, 1024, F32) for i in range(2)]
            bM = P.bank()
            for j in range(2):
                for sc in range(NT):
                    mm(psf(bM)[:, j:j + 1], Z[:, sc, j * 256: j * 256 + 128], altc, sc == 0, sc == NT - 1, sgc=True)
            for j in range(2):
                OP('act', 'activation', out=yfT[:, j, 1024:1025], in_=psf(bM)[:, j:j + 1], func=AF.Copy,
                   reads=[psf(bM)[:, j:j + 1]], writes=[yfT[:, j, 1024:1025]])
            bA = [[P.bank(), P.bank()] for _ in range(2)]
            bB = [[P.bank(), P.bank()] for _ in range(2)]
            dctr = 0
            for sc in range(NT):
                first = (sc == 0)
                last = (sc == NT - 1)
                tc_ = dring[dctr % NDR]
                dma('sp', tc_[:, 0:1024], dr['dcos'][sc * 128:(sc + 1) * 128, 0:1024], 'dft%d' % (dctr % NDR))
                dctr += 1
                for j in range(2):
                    lc = Z[:, sc, j * 256: j * 256 + 128]
                    mm(psf(bA[j][0]), lc, tc_[:, 0:512], first, last)
                    mm(psf(bA[j][1]), lc, tc_[:, 512:1024], first, last)
                ts_ = dring[dctr % NDR]
                dma('sp', ts_[:, 0:1024], dr['dnsin'][sc * 128:(sc + 1) * 128, 0:1024], 'dft%d' % (dctr % NDR))
                dctr += 1
                for j in range(2):
                    ls = Z[:, sc, j * 256 + 128: j * 256 + 256]
                    mm(psf(bB[j][0]), ls, ts_[:, 0:512], first, last)
                    mm(psf(bB[j][1]), ls, ts_[:, 512:1024], first, last)
            for j in range(2):
                for q_ in range(2):
                    tb = tmpB[q_]
                    OP('act', 'activation', out=tb, in_=psf(bB[j][q_]), func=AF.Copy, reads=[psf(bB[j][q_])], writes=[tb])
                    fwd = yfT[:, j, q_ * 512:(q_ + 1) * 512]
                    OP('dve', 'tensor_tensor', fwd, psf(bA[j][q_]), tb, ALU.add, reads=[psf(bA[j][q_]), tb], writes=[fwd])
                    i0 = 1 if q_ == 0 else 0
                    n_ = 512 - i0
                    st_ = 2048 - (q_ * 512 + i0)
                    dst = yfT[:, j, st_:st_ - n_:-1]
                    OP('dve', 'tensor_tensor', dst, psf(bA[j][q_])[:, i0:512], tb[:, i0:512], ALU.subtract,
                       reads=[psf(bA[j][q_])[:, i0:512], tb[:, i0:512]], writes=[dst])

            acc = view(X0, 16384, F32, pat="p (h t) -> p h t", h=4)
            qT = view(X0 + 16384, 4096, pat="p (c t) -> p c t", c=2)
            kT = view(X0 + 20480, 4096, pat="p (c t) -> p c t", c=2)
            vA = view(X0 + 24576, 5120, pat="p (n h e) -> p n h e", n=16, h=4)
            PT = [view(U0 + i * 1024, 1024, pat="p (h q) -> p h q", h=2) for i in range(3)]
            qb = [view(U0 + 3072 + i * 512, 512) for i in range(2)]
            rt = [view(U0 + 4096 + i * 1024, 1024, F32) for i in range(4)]
            ropec = view(U0 + 8192, 2048)
            ropes = view(U0 + 10240, 2048)
            dma('sp', ropec, dr['ropec'], 'rc')
            dma('sp', ropes, dr['ropes'], 'rs')
            rctr = [0]
            for g, dil in enumerate((1, 4, 16)):
                L = S // dil
                wq = wpiece(wcols('w_in', 256 + g * 256, 256), a=8)
                wk = wpiece(wcols('w_in', 1024 + g * 256, 256), a=8)
                wv = wpiece(wcols('w_in', 1792 + g * 256, 256), a=8)
                items = [(dstT, wsl, c, tg) for dstT, wsl in ((qT, wq), (kT, wk)) for c in range(2) for tg in range(NTG)]

                def qk_stage1(it):
                    dstT, wsl, c, tg = it
                    b = P.bank()
                    for dc in range(8):
                        mm(psf(b), wsl[:, dc, c * 128:(c + 1) * 128], hT[:, dc, tg * 512:(tg + 1) * 512], dc == 0, dc == 7)
                    k2 = rctr[0] % 2
                    rctr[0] += 1
                    OP('act', 'activation', out=qb[k2], in_=psf(b), func=AF.Copy, reads=[psf(b)], writes=[qb[k2]])
                    return (it, b, k2)

                def qk_stage2(st):
                    (dstT, wsl, c, tg), b, k2 = st
                    b2 = P.bank()
                    mm(psf(b2), perm, qb[k2], True, True)
                    t1 = rt[2 * k2]; t2 = rt[2 * k2 + 1]
                    OP('dve', 'tensor_tensor', t1, psf(b), ropec[:, tg * 512:(tg + 1) * 512], ALU.mult,
                       reads=[psf(b), ropec[:, tg * 512:(tg + 1) * 512]], writes=[t1])
                    OP('dve', 'tensor_tensor', t2, psf(b2), ropes[:, tg * 512:(tg + 1) * 512], ALU.mult,
                       reads=[psf(b2), ropes[:, tg * 512:(tg + 1) * 512]], writes=[t2])
                    n_j = 512 // dil
                    o_ = dstT[:, c, :].rearrange("p (r j) -> p r j", r=dil)[:, :, tg * n_j:(tg + 1) * n_j]
                    i1 = t1.rearrange("p (j r) -> p r j", r=dil)
                    i2 = t2.rearrange("p (j r) -> p r j", r=dil)
                    OP('pool', 'tensor_tensor', o_, i1, i2, ALU.add, reads=[t1, t2], writes=[o_])

                prev_st = None
                for it in items:
                    st_ = qk_stage1(it)
                    if prev_st is not None:
                        qk_stage2(prev_st)
                    prev_st = st_
                qk_stage2(prev_st)
                ones_col = vA[:, :, :, 64:65]
                OP('pool', 'memset', ones_col, 1.0, writes=[ones_col])
                ntl = L // 128
                for r in range(dil):
                    for jt in range(ntl):
                        n_ = r * ntl + jt
                        b = P.bank()
                        for dc in range(8):
                            lhsT = hT[:, dc, bass.ds(r + dil * 128 * jt, 128, step=dil)]
                            mm(psf(b)[:, 0:256], lhsT, wv[:, dc, :], dc == 0, dc == 7)
                        o_ = vA[:, n_, :, 0:64]
                        i_ = psf(b)[:, 0:256].rearrange("p (h e) -> p h e", h=4)
                        if n_ % 2 == 0:
                            OP('act', 'activation', out=o_, in_=i_, func=AF.Copy, reads=[i_], writes=[o_])
                        else:
                            OP('dve', 'tensor_copy', o_, i_, reads=[i_], writes=[o_])
                visits = []
                if ntl == 1:
                    for r4 in range(0, dil, 4):
                        for c in range(2):
                            visits.append(dict(r=r4, c=c, kt=0, lo=0, hi=0, mcol=128, ocol=None, first=True,
                                               fin=('g2', r4), job=('g2', r4, c)))
                else:
                    jobs = []
                    for r in range(dil):
                        for c in range(2):
                            for seg in range(ntl // 4):
                                b0 = seg * 4; b1 = b0 + 3
                                kts = list(range(max(0, b0 - 1), min(ntl - 1, b1 + 1) + 1))
                                jv = []
                                for kt in kts:
                                    lo = max(kt - 1, b0); hi = min(kt + 1, b1)
                                    jv.append(dict(r=r, c=c, kt=kt, lo=lo, hi=hi, mcol=(lo - (kt - 1)) * 128,
                                                   ocol=(lo - b0) * 128, first=(kt == kts[0]),
                                                   fin=(('seg', r, b0) if kt == kts[-1] else None), job=(r, c, seg)))
                                jobs.append(jv)
                    for a_ in range(0, len(jobs), 2):
                        ja = jobs[a_]
                        jb = jobs[a_ + 1] if a_ + 1 < len(jobs) else []
                        for i_ in range(max(len(ja), len(jb))):
                            if i_ < len(ja):
                                visits.append(ja[i_])
                            if i_ < len(jb):
                                visits.append(jb[i_])
                mflat = masks.rearrange("p m q -> p (m q)")
                obanks = None

                def emit_pv(v, obs, slot):
                    nq = (v['hi'] - v['lo'] + 1) * 128
                    if v['ocol'] is None:
                        for par in range(2):
                            hs = 2 * v['c'] + par
                            for j in range(4):
                                o_ = PSt[0:65, obs[par], j * 128:(j + 1) * 128]
                                lhsT = vA[:, v['r'] + j, hs, 0:65]
                                mm(o_, lhsT, PT[slot][:, par, j * 128:(j + 1) * 128], j == 0, j == 3, sgc=True)
                        for par in range(2):
                            hs = 2 * v['c'] + par
                            src = PSt[0:65, obs[par], :].rearrange("p (j q) -> p j q", j=4)
                            dst = acc[0:65, hs, :].rearrange("p (q r) -> p r q", r=dil)[:, v['r']:v['r'] + 4, :]
                            OP('dve', 'tensor_tensor', dst, src, dst, ALU.add, reads=[src, dst], writes=[dst])
                        return
                    for par in range(2):
                        hs = 2 * v['c'] + par
                        o_ = PSt[0:65, obs[par], v['ocol']:v['ocol'] + nq]
                        lhsT = vA[:, v['r'] * ntl + v['kt'], hs, 0:65]
                        mm(o_, lhsT, PT[slot][:, par, 0:nq], v['first'], v['fin'] is not None, sgc=True)
                    f = v['fin']
                    if f is None:
                        return
                    _, r_, b0_ = f
                    for par in range(2):
                        src = PSt[0:65, obs[par], :]
                        dst = acc[0:65, 2 * v['c'] + par, bass.ds(r_ + dil * 128 * b0_, 512, step=dil)]
                        if g == 0:
                            OP('dve', 'tensor_copy', dst, src, reads=[src], writes=[dst])
                        else:
                            OP('dve', 'tensor_tensor', dst, src, dst, ALU.add, reads=[src, dst], writes=[dst])

                LAG = 2
                pendq = []

                def flush(n_keep):
                    while len(pendq) > n_keep:
                        emit_pv(*pendq.pop(0))

                open_ob = set()
                job_ob = {}

                def abank_pair():
                    tries = 0
                    while True:
                        if P.nbank % 2 == 1:
                            P.bank()
                        b_ = P.nbank % 8
                        if b_ in open_ob:
                            P.bank(); P.bank()
                            continue
                        if any((b_ in p_[1]) for p_ in pendq) and tries < 4:
                            P.bank(); P.bank()
                            tries += 1
                            continue
                        break
                    if any((b_ in p_[1]) or ((b_ + 1) in p_[1]) for p_ in pendq):
                        flush(0)
                    P.bank(); P.bank()
                    return b_

                for vi, v in enumerate(visits):
                    c = v['c']
                    r = v['r']
                    nq = (v['hi'] - v['lo'] + 1) * 128
                    if v['first']:
                        ob_ = abank_pair()
                        job_ob[v['job']] = (ob_, ob_ + 1)
                        open_ob |= {ob_, ob_ + 1}
                    obanks = job_ob[v['job']]
                    sb = abank_pair()
                    slot = vi % 3
                    if v['ocol'] is None:
                        nq = 512
                        for par in range(2):
                            pb_ = par * 64
                            for j in range(4):
                                rr = r + j
                                lhsT = kT[pb_:pb_ + 64, c, rr * L: rr * L + 128]
                                rhs = qT[pb_:pb_ + 64, c, rr * L: rr * L + 128]
                                mm(psf(sb + par)[:, j * 128:(j + 1) * 128], lhsT, rhs, j == 0, j == 3, sgc=True)
                        mband = mflat[:, 128:256]
                        mk = mband.unsqueeze(1).unsqueeze(1).to_broadcast([128, 2, 4, 128])
                        s_in = PSt[:, sb:sb + 2, :]
                        p_out = PT[slot][:, :, 0:512]
                        OP('act', 'activation', out=p_out, in_=s_in, func=AF.Exp, scale=0.125, reads=[s_in], writes=[p_out])
                        p4 = p_out.rearrange("p h (j q) -> p h j q", j=4)
                        OP('dve', 'tensor_tensor', p4, p4, mk, ALU.mult, reads=[p_out, mband], writes=[p_out])
                    else:
                        for par in range(2):
                            pb_ = par * 64
                            lhsT = kT[pb_:pb_ + 64, c, r * L + v['kt'] * 128: r * L + (v['kt'] + 1) * 128]
                            rhs = qT[pb_:pb_ + 64, c, r * L + v['lo'] * 128: r * L + (v['hi'] + 1) * 128]
                            mm(psf(sb + par)[:, 0:nq], lhsT, rhs, True, True)
                        s_in = PSt[:, sb:sb + 2, 0:nq]
                        p_out = PT[slot][:, :, 0:nq]
                        OP('act', 'activation', out=p_out, in_=s_in, func=AF.Exp, scale=0.125, reads=[s_in], writes=[p_out])
                        mk = mflat[:, v['mcol']:v['mcol'] + nq].unsqueeze(1).to_broadcast([128, 2, nq])
                        OP('dve', 'tensor_tensor', p_out, p_out, mk, ALU.mult,
                           reads=[p_out, mflat[:, v['mcol']:v['mcol'] + nq]], writes=[p_out])
                    pendq.append((v, obanks, slot))
                    if v['fin'] is not None:
                        open_ob -= set(obanks)
                    flush(LAG)
                flush(0)
            if debug:
                dma('sp', dbg['d_acc'], acc, 'dbg')
                dma('sp', dbg['d_q'], qT, 'dbg')
                dma('sp', dbg['d_k'], kT, 'dbg')
                dma('sp', dbg['d_v'], vA, 'dbg')
            rden = view(U0, 1024, F32)
            for tg in range(NTG):
                for hs in range(4):
                    b = P.bank()
                    rhs = acc[0:65, hs, tg * 512:(tg + 1) * 512]
                    mm(PSt[0:64, b, :], sel[0:65, :], rhs, True, True)
                    OP('act', 'activation', out=rden[0:64, :], in_=PSt[0:64, b, :], func=AF.Ln, reads=[PSt[0:64, b, :]], writes=[rden[0:64, :]])
                    OP('act', 'activation', out=rden[0:64, :], in_=rden[0:64, :], func=AF.Exp, scale=-1.0, reads=[rden[0:64, :]], writes=[rden[0:64, :]])
                    o_ = yaT[0:64, hs, tg * 512:(tg + 1) * 512]
                    OP('dve', 'tensor_tensor', o_, acc[0:64, hs, tg * 512:(tg + 1) * 512], rden[0:64, :], ALU.mult,
                       reads=[acc[0:64, hs, tg * 512:(tg + 1) * 512], rden[0:64, :]], writes=[o_])

            if debug:
                dma('sp', dbg['d_yfT'], yfT, 'dbg')
                dma('sp', dbg['d_yaT'], yaT, 'dbg')
            mergedT = view(U0 + 1024, 4096, pat="p (c t) -> p c t", c=8)
            gt = [view(U0 + 5120 + i * 1024, 1024, F32) for i in range(4)]
            wfb_ = view(U0 + 9216, 2048, pat="p (k j) -> p k j", k=2)
            wab = view(U0 + 11264, 4096, pat="p (h j) -> p h j", h=4)
            dma('pool', wfb_, dr['w_fourier_branch'].rearrange("(k p) j -> p k j", p=128), 'wfa')
            dma('pool', wab[0:64], dr['w_attn_branch'].rearrange("(h e) j -> e h j", e=64), 'wab')
            def req_gates(which):
                out_ = []
                for nm_, i_ in which:
                    out_.append(wpiece(wcols('w_in', (2560 if nm_ == 'f' else 3584) + i_ * 512, 512)))
                return out_
            gcur = req_gates([('f', 0), ('a', 0), ('f', 1), ('a', 1)])
            for tg in range(NTG):
                tsl = slice(tg * 512, (tg + 1) * 512)
                wgf = [gcur[0], gcur[2]]; wga = [gcur[1], gcur[3]]
                for mc in range(8):
                    if mc == 4:
                        wmx = [wpiece(wcols('w_mix_out', i * 512, 512)) for i in range(2)]
                    b1 = P.bank()
                    for dc in range(8):
                        mm(psf(b1), wgf[mc // 4][:, dc, (mc % 4) * 128:(mc % 4 + 1) * 128], hT[:, dc, tsl], dc == 0, dc == 7)
                    b2 = P.bank()
                    for dc in range(8):
                        mm(psf(b2), wga[mc // 4][:, dc, (mc % 4) * 128:(mc % 4 + 1) * 128], hT[:, dc, tsl], dc == 0, dc == 7)
                    if mc in (1, 2, 3, 4):
                        norm_pop()
                    b3 = P.bank()
                    for j in range(2):
                        mm(psf(b3), wfb_[:, j, mc * 128:(mc + 1) * 128], yfT[:, j, tsl], j == 0, j == 1)
                    b4 = P.bank()
                    for hs in range(4):
                        mm(psf(b4), wab[0:64, hs, mc * 128:(mc + 1) * 128], yaT[0:64, hs, tsl], hs == 0, hs == 3)
                    OP('act', 'activation', out=gt[0], in_=psf(b1), func=AF.Tanh, scale=0.5, reads=[psf(b1)], writes=[gt[0]])
                    OP('act', 'activation', out=gt[1], in_=psf(b2), func=AF.Tanh, scale=0.5, reads=[psf(b2)], writes=[gt[1]])
                    OP('dve', 'scalar_tensor_tensor', gt[2], gt[0], 1.0, psf(b3), ALU.add, ALU.mult,
                       reads=[gt[0], psf(b3)], writes=[gt[2]])
                    OP('dve', 'scalar_tensor_tensor', gt[3], gt[1], 1.0, psf(b4), ALU.add, ALU.mult,
                       reads=[gt[1], psf(b4)], writes=[gt[3]])
                    OP('pool', 'tensor_tensor', mergedT[:, mc, :], gt[2], gt[3], ALU.add,
                       reads=[gt[2], gt[3]], writes=[mergedT[:, mc, :]])
                if tg + 1 < NTG:
                    gnext = req_gates([('f', 0), ('a', 0), ('f', 1)])
                else:
                    wkp_pre = [wpiece(wcols('w_ckv', half * 512, 512)) for half in range(2)]
                    wvp_pre = [wpiece(wcols('w_ckv', 1024, 512))]
                for tt in range(4):
                    t = tg * 4 + tt
                    dma('sp', xres[:, t, :], xs[t * 128:(t + 1) * 128, :], 'xr%d' % (t % 4))
                    for nh in range(2):
                        b = P.bank()
                        for fc in range(8):
                            mm(psf(b), mergedT[:, fc, tt * 128:(tt + 1) * 128], wmx[nh][:, fc, :], fc == 0, fc == 7)
                        o_ = xres[:, t, nh * 512:(nh + 1) * 512]
                        OP('dve', 'scalar_tensor_tensor', o_, psf(b), 0.5, o_, ALU.mult, ALU.add,
                           reads=[psf(b), o_], writes=[o_])
                if tg + 1 < NTG:
                    gnext += req_gates([('a', 1)])
                    gcur = gnext
                else:
                    wvp_pre.append(wpiece(wcols('w_ckv', 1024 + 512, 512)))
                norm_defer([(xres[:, tg * 4 + tt, :], 1, hT[:, :, (tg * 4 + tt) * 128:(tg * 4 + tt + 1) * 128]) for tt in range(4)])

            if debug:
                dma('sp', dbg['d_x1'], xres, 'dbg')
            KT = view(U0, 2048, pat="p (c m) -> p c m", c=8)
            Vm = view(U0 + 2048, 2048, pat="p (m f) -> p m f", m=2)
            qcT = view(U0 + 4096, 4096, pat="p (c t) -> p c t", c=8)
            PTc = [view(U0 + 8192 + i * 1024, 1024, pat="p (m t) -> p m t", m=2) for i in range(2)]
            rdc = view(U0 + 10240, 1024, F32)
            ocT = view(U0 + 11264, 4096, pat="p (c t) -> p c t", c=8)
            for half in range(2):
                wkp = wkp_pre[half]
                for fc4 in range(4):
                    fc = half * 4 + fc4
                    b = P.bank()
                    for dc in range(8):
                        mm(psf(b)[:, 0:256], wkp[:, dc, fc4 * 128:(fc4 + 1) * 128], memnT[:, dc, :], dc == 0, dc == 7)
                    OP('act', 'activation', out=KT[:, fc, :], in_=psf(b)[:, 0:256], func=AF.Copy,
                       reads=[psf(b)[:, 0:256]], writes=[KT[:, fc, :]])
            for half in range(2):
                wvp = wvp_pre[half]
                for mt in range(2):
                    b = P.bank()
                    for dc in range(8):
                        mm(psf(b), memnT[:, dc, mt * 128:(mt + 1) * 128], wvp[:, dc, :], dc == 0, dc == 7)
                    o_ = Vm[:, mt, half * 512:(half + 1) * 512]
                    OP('dve', 'tensor_copy', o_, psf(b), reads=[psf(b)], writes=[o_])
            norm_pop(4)
            wcq = [wpiece(wcols('w_cq', i * 512, 512)) for i in range(2)]
            wco = [wpiece(wcols('w_co', i * 512, 512)) for i in range(2)]
            wup_pre = [wpiece(wcols('w_up', 0, 512))]
            wdn_pre = []
            for tg in range(NTG):
                tsl = slice(tg * 512, (tg + 1) * 512)
                for fc in range(8):
                    b = P.bank()
                    for dc in range(8):
                        mm(psf(b), wcq[fc // 4][:, dc, (fc % 4) * 128:(fc % 4 + 1) * 128], hT[:, dc, tsl], dc == 0, dc == 7)
                    if fc in (1, 2, 3, 4):
                        norm_pop()
                    if fc % 2 == 0:
                        OP('act', 'activation', out=qcT[:, fc, :], in_=psf(b), func=AF.Copy, reads=[psf(b)], writes=[qcT[:, fc, :]])
                    else:
                        OP('dve', 'tensor_copy', qcT[:, fc, :], psf(b), reads=[psf(b)], writes=[qcT[:, fc, :]])
                if tg == NTG - 1:
                    wup_pre.append(wpiece(wcols('w_up', 512, 512)))
                    wdn_pre.append(req_dn(0, 0))
                for h in range(4):
                    pt = PTc[h % 2]
                    for mt in range(2):
                        b = P.bank()
                        for ec in range(2):
                            mm(psf(b), KT[:, 2 * h + ec, mt * 128:(mt + 1) * 128], qcT[:, 2 * h + ec, :], ec == 0, ec == 1)
                        OP('act', 'activation', out=pt[:, mt, :], in_=psf(b), func=AF.Exp, scale=1.0 / 16.0,
                           reads=[psf(b)], writes=[pt[:, mt, :]])
                    bd_ = P.bank()
                    for mt in range(2):
                        mm(psf(bd_), ones, pt[:, mt, :], mt == 0, mt == 1)
                    OP('act', 'activation', out=rdc, in_=psf(bd_), func=AF.Ln, reads=[psf(bd_)], writes=[rdc])
                    OP('act', 'activation', out=rdc, in_=rdc, func=AF.Exp, scale=-1.0, reads=[rdc], writes=[rdc])
                    for ec in range(2):
                        b = P.bank()
                        for mt in range(2):
                            mm(psf(b), Vm[:, mt, h * 256 + ec * 128: h * 256 + (ec + 1) * 128], pt[:, mt, :], mt == 0, mt == 1)
                        o_ = ocT[:, 2 * h + ec, :]
                        OP('dve', 'tensor_tensor', o_, psf(b), rdc, ALU.mult, reads=[psf(b), rdc], writes=[o_])
                for tt in range(4):
                    t = tg * 4 + tt
                    for nh in range(2):
                        b = P.bank()
                        for fc in range(8):
                            mm(psf(b), ocT[:, fc, tt * 128:(tt + 1) * 128], wco[nh][:, fc, :], fc == 0, fc == 7)
                        o_ = xres[:, t, nh * 512:(nh + 1) * 512]
                        OP('dve', 'tensor_tensor', o_, psf(b), o_, ALU.add, reads=[psf(b), o_], writes=[o_])
                norm_defer([(xres[:, tg * 4 + tt, :], 3, hT[:, :, (tg * 4 + tt) * 128:(tg * 4 + tt + 1) * 128]) for tt in range(4)])
            if debug:
                dma('sp', dbg['d_x2'], xres, 'dbg')

            a2T = [view(U0 + i * 4096, 4096, pat="p (c t) -> p c t", c=8) for i in range(2)]
            rl = [view(U0 + 8192 + i * 512, 512) for i in range(2)]
            ostage = [view(U0 + 9216 + i * 2048, 2048, F32) for i in range(2)]
            gfin = view(U0 + 13312, 2048, F32)
            OP('sp', 'dma_start', out=gfin, in_=dr['norm_final'].partition_broadcast(128),
               reads=[], writes=[gfin], dma_key='gfin')
            rlc = 0
            def up(tg, wup):
                tsl = slice(tg * 512, (tg + 1) * 512)
                for fc in range(8):
                    b = P.bank()
                    for dc in range(8):
                        mm(psf(b), wup[fc // 4][:, dc, (fc % 4) * 128:(fc % 4 + 1) * 128], hT[:, dc, tsl], dc == 0, dc == 7)
                    r_ = rl[rlc[0] % 2]
                    rlc[0] += 1
                    OP('act', 'activation', out=r_, in_=psf(b), func=AF.Relu, reads=[psf(b)], writes=[r_])
                    o_ = a2T[tg % 2][:, fc, :]
                    OP('pool', 'tensor_tensor', o_, r_, r_, ALU.mult, reads=[r_], writes=[o_])

            def down(tg, wdn, last):
                for tt in range(4):
                    t = tg * 4 + tt
                    for nh in range(2):
                        b = P.bank()
                        for fc in range(8):
                            mm(psf(b), a2T[tg % 2][:, fc, tt * 128:(tt + 1) * 128], wdn[nh][:, fc, :], fc == 0, fc == 7)
                        o_ = xres[:, t, nh * 512:(nh + 1) * 512]
                        OP('dve', 'tensor_tensor', o_, psf(b), o_, ALU.add, reads=[psf(b), o_], writes=[o_])
                    if last:
                        xt = xres[:, t, :]
                        kk_ = nt_ctr[0] % 4
                        nt_ctr[0] += 1
                        rstd = rms_stats(xt, xn[kk_])
                        os_ = ostage[t % 2]
                        OP('dve', 'scalar_tensor_tensor', os_, xt, rstd, gfin, ALU.mult, ALU.mult,
                           reads=[xt, rstd, gfin], writes=[os_])
                        dma('sp', out_d[b_][t * 128:(t + 1) * 128, :], os_, 'out%d' % (t % 2))

            rlc = [0]
            wup = wup_pre
            wdn = [wdn_pre[0], req_dn(0, 1)]
            for p_ in range(4):
                nxt_up = None; nxt_dn = None
                up(0, wup)
                norm_pop(4)
                for tg in range(NTG):
                    if tg + 1 < NTG:
                        up(tg + 1, wup)
                    if tg == NTG - 2 and p_ < 3:
                        nxt_up = req_up(p_ + 1)
                        nxt_dn = [req_dn(p_ + 1, 0)]
                    down(tg, wdn, p_ == 3)
                    if p_ == 3 and tg == 1 and b_ + 1 < nseq:
                        for t_ in range(4):
                            dma('sp', xin[t_], dr['x'][b_ + 1][t_ * 128:(t_ + 1) * 128, :], 'xin%d' % t_)
                        prefetched.add(b_ + 1)
                if p_ < 3:
                    nxt_dn.append(req_dn(p_ + 1, 1))
                    wup, wdn = nxt_up, nxt_dn

        P.emit(sem_alloc)
    return nc


_CACHE = {}


def kernel(**inputs):
    nseq = SEQ_PER_CORE
    consts = host_constants()
    x = np.ascontiguousarray(inputs['x'], dtype=np.float32)
    mem = np.ascontiguousarray(inputs['mem'], dtype=np.float32)
    B = x.shape[0]
    n_cores = B // nseq
    if 'nc' not in _CACHE:
        _CACHE['nc'] = build_program(nseq)
    nc = _CACHE['nc']
    shared = {}
    for name, shp in W_SPECS:
        a = np.asarray(inputs[name], dtype=np.float32)
        shared[name] = np.ascontiguousarray(a.reshape(shp))
    for name, shp, dt in CONST_SPECS:
        shared[name] = consts[name]
    in_maps = []
    for c in range(n_cores):
        m = dict(shared)
        m['x'] = x[c * nseq:(c + 1) * nseq]
        m['mem'] = mem[c * nseq:(c + 1) * nseq]
        in_maps.append(m)
    res = run_bass_kernel_spmd(nc, in_maps, core_ids=list(range(n_cores)))
    out = np.concatenate([np.asarray(r['out']) for r in res.results], axis=0)
    return out.astype(np.float32)
```

```python
import numpy as np
import ml_dtypes
import concourse.bass as bass
import concourse.mybir as mybir
from concourse.bass_utils import run_bass_kernel_spmd

F32 = mybir.dt.float32
BF16 = mybir.dt.bfloat16
AF = mybir.ActivationFunctionType
ALU = mybir.AluOpType
DSZ = {F32: 4, BF16: 2}

S = 2048
D = 1024
NT = 16
NTG = 4
NEG = -30000.0
N_CORES = 8
SEQ_PER_CORE = 2


class Prog:
    BLK = 256

    def __init__(self, nc):
        self.nc = nc
        self.ops = []
        self.lastw = {}
        self.readers = {}
        self.dma_prev = {}
        self.nbank = 0

    def res(self, ap):
        sp = str(ap.space)
        if 'SB' in sp:
            kind = 'sb'
        elif 'PS' in sp.upper():
            kind = 'ps'
        else:
            return []
        pat = ap.ap
        pstep = pat[0][0]
        free_off = ap.offset - ap.base_partition() * pstep
        sz = DSZ[ap.dtype]
        dims = [(s, c) for (s, c) in pat[1:]]
        if not dims:
            dims = [(1, 1)]
        inner_s, inner_c = dims[-1]
        outer = dims[:-1]
        nouter = 1
        for s, c in outer:
            nouter *= c
        ivals = []
        if nouter <= 512:
            starts = [0]
            for s, c in outer:
                starts = [b + i * s for b in starts for i in range(c)]
            for st in starts:
                lo = free_off + st
                if inner_s < 0:
                    lo -= (inner_c - 1) * (-inner_s)
                hi = lo + (inner_c - 1) * abs(inner_s) + 1
                ivals.append((lo * sz, hi * sz))
        else:
            lo = free_off
            hi = free_off + sum((c - 1) * abs(s) for s, c in dims) + 1
            ivals.append((lo * sz, hi * sz))
        out = set()
        if kind == 'ps':
            for lo, hi in ivals:
                for b in range(lo // 2048, (hi - 1) // 2048 + 1):
                    out.add(('ps', b))
        else:
            for lo, hi in ivals:
                for b in range(lo // self.BLK, (hi - 1) // self.BLK + 1):
                    out.add(b)
        return out

    def op(self, eng, name, *args, reads=(), writes=(), dma_key=None, **kw):
        i = len(self.ops)
        rset, wset = set(), set()
        for a in reads:
            r = self.res(a)
            for k in r:
                if isinstance(k, tuple):
                    wset.add(k)
                else:
                    rset.add(k)
        for a in writes:
            wset |= set(self.res(a))
        raw, oth = set(), set()
        for k in rset:
            j = self.lastw.get(k)
            if j is not None:
                raw.add(j)
        for k in wset:
            j = self.lastw.get(k)
            if j is not None:
                (raw if isinstance(k, tuple) else oth).add(j)
            for j in self.readers.get(k, ()):
                oth.add(j)
        if dma_key is not None:
            j = self.dma_prev.get(dma_key)
            if j is not None:
                raw.add(j)
            self.dma_prev[dma_key] = i
        for k in rset:
            self.readers.setdefault(k, []).append(i)
        for k in wset:
            self.lastw[k] = i
            self.readers[k] = []
        deps = set()
        for j in raw | oth:
            oj = self.ops[j]
            if oj['dma'] is None and oj['eng'] == eng:
                if eng == 'pe':
                    continue
            deps.add(j)
        self.ops.append(dict(eng=eng, name=name, args=args, kw=kw, deps=deps, dma=dma_key))
        return i

    def bank(self):
        b = self.nbank % 8
        self.nbank += 1
        return b

    def emit(self, sem_alloc):
        nc = self.nc
        engs = {'pe': nc.tensor, 'act': nc.scalar, 'dve': nc.vector, 'pool': nc.gpsimd, 'sp': nc.sync}
        n = len(self.ops)
        need = [False] * n
        for o in self.ops:
            for j in o['deps']:
                need[j] = True
        esem = {e: sem_alloc('e_' + e) for e in ('pe', 'act', 'dve', 'pool')}
        dsem = {}
        ecnt = {e: 0 for e in esem}
        dcnt = {}
        sig = [None] * n
        for i, o in enumerate(self.ops):
            if o['dma'] is not None:
                k = o['dma']
                if k not in dsem:
                    dsem[k] = sem_alloc('d_' + k)
                    dcnt[k] = 0
                dcnt[k] += 16
                sig[i] = (dsem[k], dcnt[k])
            elif need[i]:
                e = o['eng']
                ecnt[e] += 1
                sig[i] = (esem[e], ecnt[e])
        seen = {e: {} for e in engs}
        for i, o in enumerate(self.ops):
            e = o['eng']
            eo = engs[e]
            want = {}
            for j in o['deps']:
                sm, v = sig[j]
                if want.get(sm.num, (None, 0))[1] < v:
                    want[sm.num] = (sm, v)
            for num, (sm, v) in sorted(want.items()):
                if seen[e].get(num, 0) >= v:
                    continue
                eo.wait_ge(sm, v)
                seen[e][num] = v
            ins = getattr(eo, o['name'])(*o['args'], **o['kw'])
            if sig[i] is not None:
                ins.then_inc(sig[i][0], 16 if o['dma'] is not None else 1)
        for k, sm in dsem.items():
            if seen['sp'].get(sm.num, 0) < dcnt[k]:
                nc.sync.wait_ge(sm, dcnt[k])


def host_constants():
    bf = ml_dtypes.bfloat16
    c = {}
    c['ident'] = np.eye(128, dtype=np.float32).astype(bf)
    c['ones'] = np.ones((128, 128), np.float32).astype(bf)
    perm = np.zeros((128, 128), np.float32)
    for p in range(128):
        e = p % 64
        if e < 8:
            perm[p + 8, p] = 1.0
        elif e < 16:
            perm[p - 8, p] = 1.0
    c['perm'] = perm.astype(bf)
    kp = np.arange(128)[:, None]
    q = np.arange(128)[None, :]
    masks = np.stack([
        np.where(q - kp >= 64, 1.0, 0.0),
        np.where(np.abs(kp - q) <= 64, 1.0, 0.0),
        np.where(kp - q >= 64, 1.0, 0.0),
    ]).astype(np.float32)
    c['masks'] = masks.astype(bf)
    i64 = np.arange(64)
    ang = 2 * np.pi * np.outer(i64, i64) / 64.0
    bdc = np.zeros((128, 128)); bds = np.zeros((128, 128))
    for g in range(2):
        bdc[g * 64:(g + 1) * 64, g * 64:(g + 1) * 64] = np.cos(ang)
        bds[g * 64:(g + 1) * 64, g * 64:(g + 1) * 64] = np.sin(ang)
    c['bdcos'] = bdc.astype(np.float32).astype(bf)
    c['bdsin'] = bds.astype(np.float32).astype(bf)
    s = np.arange(S, dtype=np.int64)
    ph = (np.outer(s, s) % S).astype(np.float64) * (2 * np.pi / S)
    nrm = 1.0 / np.sqrt(S * 64.0)
    c['dcos'] = (np.cos(ph) * nrm).astype(np.float32).astype(bf)
    c['dnsin'] = (-np.sin(ph) * nrm).astype(np.float32).astype(bf)
    half = 8
    inv_freq = (np.float32(500000.0) ** (-np.arange(half, dtype=np.float32) / np.float32(half))).astype(np.float32)
    angr = (np.arange(S, dtype=np.float32)[:, None] * inv_freq[None, :]).astype(np.float32)
    cosr = np.cos(angr).astype(np.float32).T
    sinr = np.sin(angr).astype(np.float32).T
    rc = np.ones((128, S), np.float32)
    rs = np.zeros((128, S), np.float32)
    for p in range(128):
        e = p % 64
        if e < 8:
            rc[p] = cosr[e]; rs[p] = -sinr[e]
        elif e < 16:
            rc[p] = cosr[e - 8]; rs[p] = sinr[e - 8]
    c['ropec'] = rc.astype(bf)
    c['ropes'] = rs.astype(bf)
    c['alt'] = (((-1.0) ** np.arange(128))[:, None] * nrm).astype(np.float32).astype(bf)
    sel = np.zeros((128, 64), np.float32)
    sel[64, :] = 1.0
    c['sel'] = sel
    return c


CONST_SPECS = [
    ('ident', [128, 128], BF16), ('ones', [128, 128], BF16), ('perm', [128, 128], BF16),
    ('masks', [3, 128, 128], BF16), ('bdcos', [128, 128], BF16), ('bdsin', [128, 128], BF16),
    ('dcos', [S, S], BF16), ('dnsin', [S, S], BF16),
    ('ropec', [128, S], BF16), ('ropes', [128, S], BF16), ('sel', [128, 64], F32), ('alt', [128, 1], BF16),
]
W_SPECS = [
    ('norm_mix', [D]), ('w_in', [D, 4608]), ('fourier_mix', [4, 64, 64]), ('w_fourier_branch', [256, D]),
    ('w_attn_branch', [256, D]), ('w_mix_out', [D, D]), ('norm_cross', [D]), ('norm_mem', [D]),
    ('w_cq', [D, D]), ('w_ckv', [D, 2 * D]), ('w_co', [D, D]), ('norm_mlp', [D]),
    ('w_up', [D, 4 * D]), ('w_down', [4 * D, D]), ('norm_final', [D]),
]


def build_program(nseq, debug=False):
    nc = bass.Bass("TRN2", target_bir_lowering=False)
    dr = {}
    dr['x'] = nc.dram_tensor("x", [nseq, S, D], F32, kind="ExternalInput").ap()
    dr['mem'] = nc.dram_tensor("mem", [nseq, 256, D], F32, kind="ExternalInput").ap()
    for name, shp in W_SPECS:
        dr[name] = nc.dram_tensor(name, shp, F32, kind="ExternalInput").ap()
    for name, shp, dt in CONST_SPECS:
        dr[name] = nc.dram_tensor(name, shp, dt, kind="ExternalInput").ap()
    out_d = nc.dram_tensor("out", [nseq, S, D], F32, kind="ExternalOutput").ap()
    dbg = {}
    if debug:
        for nm, shp, dt in (('d_hT', [128, 8, 2048], BF16), ('d_yfT', [128, 2, 2048], BF16), ('d_yaT', [128, 4, 2048], BF16),
                            ('d_acc', [128, 4, 2048], F32), ('d_x1', [128, 16, 1024], F32), ('d_x2', [128, 16, 1024], F32),
                            ('d_q', [128, 2, 2048], BF16), ('d_k', [128, 2, 2048], BF16), ('d_v', [128, 16, 4, 80], BF16)):
            dbg[nm] = nc.dram_tensor(nm, shp, dt, kind="ExternalOutput").ap()

    OFF = {}
    cur = [0]

    def alloc(name, n):
        n = (n + 127) // 128 * 128
        OFF[name] = cur[0]
        cur[0] += n
        return OFF[name]

    alloc('const', 2816)
    alloc('memnT', 2048)
    alloc('wring', 5 * 4096)
    alloc('hT', 8 * 2048)
    alloc('temps', 4096)
    alloc('yfT', 2 * 2048)
    alloc('yaT', 4 * 2048)
    alloc('X', 16 * 1024 * 2)
    alloc('U', 15360)
    TOTAL = cur[0]

    sems = []
    import contextlib
    with contextlib.ExitStack() as es:
        A = es.enter_context(nc.sbuf_tensor("arena", [128, TOTAL], BF16))
        PSt = es.enter_context(nc.psum_tensor("ps", [128, 8, 512], F32))

        def sem_alloc(name):
            s_ = es.enter_context(nc.semaphore(name))
            sems.append(s_)
            return s_

        P = Prog(nc)

        def view(off, cnt_, dt=BF16, pat=None, **kw):
            v = A[:, off:off + cnt_]
            if dt == F32:
                v = v.bitcast(F32)
            if pat:
                v = v.rearrange(pat, **kw)
            return v

        def psf(b):
            return PSt[:, b, :]

        def psb(b):
            return PSt[:, b, :].bitcast(BF16)

        OP = P.op

        def mm(out, lhsT, rhs, start, stop, sgc=False):
            if sgc:
                OP('pe', 'matmul', out, lhsT, rhs, start=start, stop=stop, skip_group_check=True, reads=[lhsT, rhs], writes=[out])
            else:
                OP('pe', 'matmul', out, lhsT, rhs, start=start, stop=stop, reads=[lhsT, rhs], writes=[out])

        def dma(q, out, in_, key):
            OP(q, 'dma_start', out=out, in_=in_, reads=[in_], writes=[out], dma_key=key)

        c0 = OFF['const']
        ident = view(c0 + 0, 128)
        ones = view(c0 + 128, 128)
        perm = view(c0 + 256, 128)
        masks = view(c0 + 384, 384, pat="p (m q) -> p m q", m=3)
        bdcos = view(c0 + 768, 128)
        bdsin = view(c0 + 896, 128)
        Wz = view(c0 + 1024, 512, pat="p (j c) -> p j c", j=2)
        sel = view(c0 + 1536, 128, F32)
        gT = view(c0 + 1664, 128, F32, pat="p (n c) -> p n c", n=8)
        negh = view(c0 + 1792, 32, F32)
        ST = view(c0 + 1920, 384, F32)
        fmst = view(c0 + 2304, 128, pat="p (j d) -> p j d", j=2)
        fm32 = view(c0 + 2432, 256, F32, pat="p (j d) -> p j d", j=2)

        dma('sp', ident, dr['ident'], 'c')
        dma('sp', ones, dr['ones'], 'c1')
        dma('sp', perm, dr['perm'], 'c2')
        dma('sp', masks, dr['masks'].rearrange("m p q -> p m q"), 'c3')
        dma('sp', bdcos, dr['bdcos'], 'c4')
        dma('sp', bdsin, dr['bdsin'], 'c5')
        dma('sp', sel, dr['sel'], 'c6')
        altc = view(c0 + 2688, 2)[:, 0:1]
        dma('sp', altc, dr['alt'], 'c7')
        GIDX = {'norm_mix': 0, 'norm_cross': 1, 'norm_mem': 2, 'norm_mlp': 3}
        for nm, gi in GIDX.items():
            OP('pool', 'dma_start', out=gT[:, gi, :], in_=dr[nm].rearrange("(c p) -> p c", p=128),
               reads=[], writes=[gT[:, gi, :]], dma_key='g%d' % gi, allow_slow_non_contiguous=True)
        OP('dve', 'memset', negh, -0.5, writes=[negh])
        OP('dve', 'memset', negh[:, 1:2], 1e-6, writes=[negh[:, 1:2]])
        OP('dve', 'memset', Wz, 0.0, writes=[Wz])
        for j in range(2):
            dma('sp', fm32[:, j, :], dr['fourier_mix'][2 * j:2 * j + 2].rearrange("g c d -> (g c) d"), 'fm%d' % j)
            OP('dve', 'tensor_copy', fmst[:, j, :], fm32[:, j, :], reads=[fm32[:, j, :]], writes=[fmst[:, j, :]])
            for t, bd in enumerate((bdcos, bdsin)):
                b = P.bank()
                mm(psf(b)[:, 0:64], bd, fmst[:, j, :], True, True)
                for g in range(2):
                    o_ = Wz[g * 64:(g + 1) * 64, j, t * 128 + g * 64: t * 128 + (g + 1) * 64]
                    i_ = psf(b)[g * 64:(g + 1) * 64, 0:64]
                    OP('dve', 'tensor_copy', o_, i_, reads=[i_], writes=[o_])

        hT = view(OFF['hT'], 8 * 2048, pat="p (c t) -> p c t", c=8)
        t0 = OFF['temps']
        xn = [view(t0 + i * 1024, 1024) for i in range(4)]
        misc = t0 + 3072
        yfT = view(OFF['yfT'], 2 * 2048, pat="p (c t) -> p c t", c=2)
        yaT = view(OFF['yaT'], 4 * 2048, pat="p (h t) -> p h t", h=4)
        X0 = OFF['X']
        U0 = OFF['U']
        xres = view(X0, 16 * 2048, F32, pat="p (n d) -> p n d", n=16)
        NSLOT = 5
        wslots = [view(OFF['wring'] + i * 4096, 4096) for i in range(NSLOT)]
        wstate = {'n': 0}

        def wpiece(src3, parts=128, a=8):
            i = wstate['n'] % NSLOT
            wstate['n'] += 1
            dst = wslots[i][0:parts, :].rearrange("p (a b) -> p a b", a=a)[:, :, 0:src3.shape[-1]]
            dma('pool', dst, src3, 'w%d' % i)
            return dst

        def wcols(name, c0_, ncol, nk=8):
            return dr[name][:, c0_:c0_ + ncol].rearrange("(k p) j -> p k j", p=128)

        stat_ctr = [0]

        def rms_stats(xt, junk):
            s_ = stat_ctr[0] % 32
            stat_ctr[0] += 1
            ss = ST[:, s_:s_ + 1]
            ms = ST[:, 64 + s_:64 + s_ + 1]
            rstd = ST[:, 128 + s_:128 + s_ + 1]
            OP('act', 'activation', out=junk, in_=xt, func=AF.Square, accum_out=ss, reads=[xt], writes=[junk, ss])
            OP('act', 'activation', out=ms, in_=ss, func=AF.Identity, scale=1.0 / D, bias=negh[:, 1:2], reads=[ss, negh[:, 1:2]], writes=[ms])
            OP('pool', 'tensor_tensor', rstd, ms, negh[:, 0:1], ALU.pow, reads=[ms, negh[:, 0:1]], writes=[rstd])
            return rstd

        nt_ctr = [0]

        def norm_A(xt):
            k = nt_ctr[0] % 4
            nt_ctr[0] += 1
            rstd = rms_stats(xt, xn[k])
            OP('dve', 'tensor_scalar_mul', xn[k], xt, rstd, reads=[xt, rstd], writes=[xn[k]])
            return k

        def norm_B(k, gi, dst):
            b = P.bank()
            pb = psb(b)
            for dc in range(8):
                o_ = pb[:, dc * 128:(dc + 1) * 128]
                i_ = xn[k][:, dc * 128:(dc + 1) * 128]
                OP('pe', 'transpose', o_, i_, ident, reads=[i_, ident], writes=[o_])
            gb = gT[:, gi, :].unsqueeze(2).to_broadcast([128, 8, 128])
            pin = pb.rearrange("p (c t) -> p c t", c=8)
            OP('dve', 'tensor_tensor', dst, pin, gb, ALU.mult, reads=[pin, gT[:, gi, :]], writes=[dst])

        def norm_many(items):
            prev = None
            for xt, gi, dst, pre in items:
                if pre is not None:
                    pre()
                k = norm_A(xt)
                if prev is not None:
                    norm_B(*prev)
                prev = (k, gi, dst)
            norm_B(*prev)

        pendB = []

        def norm_defer(items):
            assert not pendB
            for xt, gi, dst in items:
                pendB.append((norm_A(xt), gi, dst))

        def norm_pop(n=1):
            for _ in range(n):
                if pendB:
                    norm_B(*pendB.pop(0))

        def req_up(p_):
            return [wpiece(wcols('w_up', p_ * 1024 + i * 512, 512)) for i in range(2)]

        def req_dn(p_, i):
            return wpiece(dr['w_down'][p_ * 1024:(p_ + 1) * 1024, i * 512:(i + 1) * 512].rearrange("(k p) j -> p k j", p=128))

        prefetched = set()
        for b_ in range(nseq):
            xs = dr['x'][b_]
            xin = [view(X0 + i * 2048, 2048, F32) for i in range(4)]
            def mk_pre(t):
                if t < 4 and b_ in prefetched:
                    return None
                return lambda: dma('sp', xin[t % 4], xs[t * 128:(t + 1) * 128, :], 'xin%d' % (t % 4))
            memnT = view(OFF['memnT'], 2048, pat="p (c m) -> p c m", c=8)
            memst = [view(X0 + 8192 + i * 2048, 2048, F32) for i in range(2)]

            def mk_mpre(mt):
                return lambda: dma('sp', memst[mt], dr['mem'][b_][mt * 128:(mt + 1) * 128, :], 'mem%d' % mt)
            norm_many([(xin[t % 4], 0, hT[:, :, t * 128:(t + 1) * 128], mk_pre(t)) for t in range(NT)]
                      + [(memst[mt], 2, memnT[:, :, mt * 128:(mt + 1) * 128], mk_mpre(mt)) for mt in range(2)])

            if debug:
                dma('sp', dbg['d_hT'], hT, 'dbg')
            FB = X0 + 8192
            uT = view(FB, 4096, pat="p (c t) -> p c t", c=2)
            Z = view(FB + 4096, 8192, pat="p (n z) -> p n z", n=16)
            NDR = 6
            dring = [view(FB + 12288 + i * 1152, 1152) for i in range(NDR)]
            wu = wpiece(wcols('w_in', 0, 256))
            for tg in range(NTG):
                for j in range(2):
                    b = P.bank()
                    for dc in range(8):
                        mm(psf(b), wu[:, dc, j * 128:(j + 1) * 128], hT[:, dc, tg * 512:(tg + 1) * 512], dc == 0, dc == 7)
                    o_ = uT[:, j, tg * 512:(tg + 1) * 512]
                    OP('act', 'activation', out=o_, in_=psf(b), func=AF.Copy, reads=[psf(b)], writes=[o_])
            for t in range(NT):
                b = P.bank()
                for j in range(2):
                    mm(psf(b)[:, j * 256:(j + 1) * 256], uT[:, j, t * 128:(t + 1) * 128], Wz[:, j, :], True, True)
                if t % 2 == 0:
                    OP('dve', 'tensor_copy', Z[:, t, :], psf(b), reads=[psf(b)], writes=[Z[:, t, :]])
                else:
                    OP('act', 'activation', out=Z[:, t, :], in_=psf(b), func=AF.Copy, reads=[psf(b)], writes=[Z[:, t, :]])
            tmpB = [view(FB + 12288 + NDR * 1152 + i * 1024, 1024, F32) for i in range(2)]
            bM = P.bank()
            for j in range(2):
                for sc in range(NT):
                    mm(psf(bM)[:, j:j + 1], Z[:, sc, j * 256: j * 256 + 128], altc, sc == 0, sc == NT - 1, sgc=True)
            for j in range(2):
                OP('act', 'activation', out=yfT[:, j, 1024:1025], in_=psf(bM)[:, j:j + 1], func=AF.Copy,
                   reads=[psf(bM)[:, j:j + 1]], writes=[yfT[:, j, 1024:1025]])
            bA = [[P.bank(), P.bank()] for _ in range(2)]
            bB = [[P.bank(), P.bank()] for _ in range(2)]
            dctr = 0
            for sc in range(NT):
                first = (sc == 0)
                last = (sc == NT - 1)
                tc_ = dring[dctr % NDR]
                dma('sp', tc_[:, 0:1024], dr['dcos'][sc * 128:(sc + 1) * 128, 0:1024], 'dft%d' % (dctr % NDR))
                dctr += 1
                for j in range(2):
                    lc = Z[:, sc, j * 256: j * 256 + 128]
                    mm(psf(bA[j][0]), lc, tc_[:, 0:512], first, last)
                    mm(psf(bA[j][1]), lc, tc_[:, 512:1024], first, last)
                ts_ = dring[dctr % NDR]
                dma('sp', ts_[:, 0:1024], dr['dnsin'][sc * 128:(sc + 1) * 128, 0:1024], 'dft%d' % (dctr % NDR))
                dctr += 1
                for j in range(2):
                    ls = Z[:, sc, j * 256 + 128: j * 256 + 256]
                    mm(psf(bB[j][0]), ls, ts_[:, 0:512], first, last)
                    mm(psf(bB[j][1]), ls, ts_[:, 512:1024], first, last)
            for j in range(2):
                for q_ in range(2):
                    tb = tmpB[q_]
                    OP('act', 'activation', out=tb, in_=psf(bB[j][q_]), func=AF.Copy, reads=[psf(bB[j][q_])], writes=[tb])
                    fwd = yfT[:, j, q_ * 512:(q_ + 1) * 512]
                    OP('dve', 'tensor_tensor', fwd, psf(bA[j][q_]), tb, ALU.add, reads=[psf(bA[j][q_]), tb], writes=[fwd])
                    i0 = 1 if q_ == 0 else 0
                    n_ = 512 - i0
                    st_ = 2048 - (q_ * 512 + i0)
                    dst = yfT[:, j, st_:st_ - n_:-1]
                    OP('dve', 'tensor_tensor', dst, psf(bA[j][q_])[:, i0:512], tb[:, i0:512], ALU.subtract,
                       reads=[psf(bA[j][q_])[:, i0:512], tb[:, i0:512]], writes=[dst])

            acc = view(X0, 16384, F32, pat="p (h t) -> p h t", h=4)
            qT = view(X0 + 16384, 4096, pat="p (c t) -> p c t", c=2)
            kT = view(X0 + 20480, 4096, pat="p (c t) -> p c t", c=2)
            vA = view(X0 + 24576, 5120, pat="p (n h e) -> p n h e", n=16, h=4)
            PT = [view(U0 + i * 1024, 1024, pat="p (h q) -> p h q", h=2) for i in range(3)]
            qb = [view(U0 + 3072 + i * 512, 512) for i in range(2)]
            rt = [view(U0 + 4096 + i * 1024, 1024, F32) for i in range(4)]
            ropec = view(U0 + 8192, 2048)
            ropes = view(U0 + 10240, 2048)
            dma('sp', ropec, dr['ropec'], 'rc')
            dma('sp', ropes, dr['ropes'], 'rs')
            rctr = [0]
            for g, dil in enumerate((1, 4, 16)):
                L = S // dil
                wq = wpiece(wcols('w_in', 256 + g * 256, 256), a=8)
                wk = wpiece(wcols('w_in', 1024 + g * 256, 256), a=8)
                wv = wpiece(wcols('w_in', 1792 + g * 256, 256), a=8)
                items = [(dstT, wsl, c, tg) for dstT, wsl in ((qT, wq), (kT, wk)) for c in range(2) for tg in range(NTG)]

                def qk_stage1(it):
                    dstT, wsl, c, tg = it
                    b = P.bank()
                    for dc in range(8):
                        mm(psf(b), wsl[:, dc, c * 128:(c + 1) * 128], hT[:, dc, tg * 512:(tg + 1) * 512], dc == 0, dc == 7)
                    k2 = rctr[0] % 2
                    rctr[0] += 1
                    OP('act', 'activation', out=qb[k2], in_=psf(b), func=AF.Copy, reads=[psf(b)], writes=[qb[k2]])
                    return (it, b, k2)

                def qk_stage2(st):
                    (dstT, wsl, c, tg), b, k2 = st
                    b2 = P.bank()
                    mm(psf(b2), perm, qb[k2], True, True)
                    t1 = rt[2 * k2]; t2 = rt[2 * k2 + 1]
                    OP('dve', 'tensor_tensor', t1, psf(b), ropec[:, tg * 512:(tg + 1) * 512], ALU.mult,
                       reads=[psf(b), ropec[:, tg * 512:(tg + 1) * 512]], writes=[t1])
                    OP('dve', 'tensor_tensor', t2, psf(b2), ropes[:, tg * 512:(tg + 1) * 512], ALU.mult,
                       reads=[psf(b2), ropes[:, tg * 512:(tg + 1) * 512]], writes=[t2])
                    n_j = 512 // dil
                    o_ = dstT[:, c, :].rearrange("p (r j) -> p r j", r=dil)[:, :, tg * n_j:(tg + 1) * n_j]
                    i1 = t1.rearrange("p (j r) -> p r j", r=dil)
                    i2 = t2.rearrange("p (j r) -> p r j", r=dil)
                    OP('pool', 'tensor_tensor', o_, i1, i2, ALU.add, reads=[t1, t2], writes=[o_])

                prev_st = None
                for it in items:
                    st_ = qk_stage1(it)
                    if prev_st is not None:
                        qk_stage2(prev_st)
                    prev_st = st_
                qk_stage2(prev_st)
                ones_col = vA[:, :, :, 64:65]
                OP('pool', 'memset', ones_col, 1.0, writes=[ones_col])
                ntl = L // 128
                for r in range(dil):
                    for jt in range(ntl):
                        n_ = r * ntl + jt
                        b = P.bank()
                        for dc in range(8):
                            lhsT = hT[:, dc, bass.ds(r + dil * 128 * jt, 128, step=dil)]
                            mm(psf(b)[:, 0:256], lhsT, wv[:, dc, :], dc == 0, dc == 7)
                        o_ = vA[:, n_, :, 0:64]
                        i_ = psf(b)[:, 0:256].rearrange("p (h e) -> p h e", h=4)
                        if n_ % 2 == 0:
                            OP('act', 'activation', out=o_, in_=i_, func=AF.Copy, reads=[i_], writes=[o_])
                        else:
                            OP('dve', 'tensor_copy', o_, i_, reads=[i_], writes=[o_])
                visits = []
                if ntl == 1:
                    for r4 in range(0, dil, 4):
                        for c in range(2):
                            visits.append(dict(r=r4, c=c, kt=0, lo=0, hi=0, mcol=128, ocol=None, first=True,
                                               fin=('g2', r4), job=('g2', r4, c)))
                else:
                    jobs = []
                    for r in range(dil):
                        for c in range(2):
                            for seg in range(ntl // 4):
                                b0 = seg * 4; b1 = b0 + 3
                                kts = list(range(max(0, b0 - 1), min(ntl - 1, b1 + 1) + 1))
                                jv = []
                                for kt in kts:
                                    lo = max(kt - 1, b0); hi = min(kt + 1, b1)
                                    jv.append(dict(r=r, c=c, kt=kt, lo=lo, hi=hi, mcol=(lo - (kt - 1)) * 128,
                                                   ocol=(lo - b0) * 128, first=(kt == kts[0]),
                                                   fin=(('seg', r, b0) if kt == kts[-1] else None), job=(r, c, seg)))
                                jobs.append(jv)
                    for a_ in range(0, len(jobs), 2):
                        ja = jobs[a_]
                        jb = jobs[a_ + 1] if a_ + 1 < len(jobs) else []
                        for i_ in range(max(len(ja), len(jb))):
                            if i_ < len(ja):
                                visits.append(ja[i_])
                            if i_ < len(jb):
                                visits.append(jb[i_])
                mflat = masks.rearrange("p m q -> p (m q)")
                obanks = None

                def emit_pv(v, obs, slot):
                    nq = (v['hi'] - v['lo'] + 1) * 128
                    if v['ocol'] is None:
                        for par in range(2):
                            hs = 2 * v['c'] + par
                            for j in range(4):
                                o_ = PSt[0:65, obs[par], j * 128:(j + 1) * 128]
                                lhsT = vA[:, v['r'] + j, hs, 0:65]
                                mm(o_, lhsT, PT[slot][:, par, j * 128:(j + 1) * 128], j == 0, j == 3, sgc=True)
                        for par in range(2):
                            hs = 2 * v['c'] + par
                            src = PSt[0:65, obs[par], :].rearrange("p (j q) -> p j q", j=4)
                            dst = acc[0:65, hs, :].rearrange("p (q r) -> p r q", r=dil)[:, v['r']:v['r'] + 4, :]
                            OP('dve', 'tensor_tensor', dst, src, dst, ALU.add, reads=[src, dst], writes=[dst])
                        return
                    for par in range(2):
                        hs = 2 * v['c'] + par
                        o_ = PSt[0:65, obs[par], v['ocol']:v['ocol'] + nq]
                        lhsT = vA[:, v['r'] * ntl + v['kt'], hs, 0:65]
                        mm(o_, lhsT, PT[slot][:, par, 0:nq], v['first'], v['fin'] is not None, sgc=True)
                    f = v['fin']
                    if f is None:
                        return
                    _, r_, b0_ = f
                    for par in range(2):
                        src = PSt[0:65, obs[par], :]
                        dst = acc[0:65, 2 * v['c'] + par, bass.ds(r_ + dil * 128 * b0_, 512, step=dil)]
                        if g == 0:
                            OP('dve', 'tensor_copy', dst, src, reads=[src], writes=[dst])
                        else:
                            OP('dve', 'tensor_tensor', dst, src, dst, ALU.add, reads=[src, dst], writes=[dst])

                LAG = 2
                pendq = []

                def flush(n_keep):
                    while len(pendq) > n_keep:
                        emit_pv(*pendq.pop(0))

                open_ob = set()
                job_ob = {}

                def abank_pair():
                    tries = 0
                    while True:
                        if P.nbank % 2 == 1:
                            P.bank()
                        b_ = P.nbank % 8
                        if b_ in open_ob:
                            P.bank(); P.bank()
                            continue
                        if any((b_ in p_[1]) for p_ in pendq) and tries < 4:
                            P.bank(); P.bank()
                            tries += 1
                            continue
                        break
                    if any((b_ in p_[1]) or ((b_ + 1) in p_[1]) for p_ in pendq):
                        flush(0)
                    P.bank(); P.bank()
                    return b_

                for vi, v in enumerate(visits):
                    c = v['c']
                    r = v['r']
                    nq = (v['hi'] - v['lo'] + 1) * 128
                    if v['first']:
                        ob_ = abank_pair()
                        job_ob[v['job']] = (ob_, ob_ + 1)
                        open_ob |= {ob_, ob_ + 1}
                    obanks = job_ob[v['job']]
                    sb = abank_pair()
                    slot = vi % 3
                    if v['ocol'] is None:
                        nq = 512
                        for par in range(2):
                            pb_ = par * 64
                            for j in range(4):
                                rr = r + j
                                lhsT = kT[pb_:pb_ + 64, c, rr * L: rr * L + 128]
                                rhs = qT[pb_:pb_ + 64, c, rr * L: rr * L + 128]
                                mm(psf(sb + par)[:, j * 128:(j + 1) * 128], lhsT, rhs, j == 0, j == 3, sgc=True)
                        mband = mflat[:, 128:256]
                        mk = mband.unsqueeze(1).unsqueeze(1).to_broadcast([128, 2, 4, 128])
                        s_in = PSt[:, sb:sb + 2, :]
                        p_out = PT[slot][:, :, 0:512]
                        OP('act', 'activation', out=p_out, in_=s_in, func=AF.Exp, scale=0.125, reads=[s_in], writes=[p_out])
                        p4 = p_out.rearrange("p h (j q) -> p h j q", j=4)
                        OP('dve', 'tensor_tensor', p4, p4, mk, ALU.mult, reads=[p_out, mband], writes=[p_out])
                    else:
                        for par in range(2):
                            pb_ = par * 64
                            lhsT = kT[pb_:pb_ + 64, c, r * L + v['kt'] * 128: r * L + (v['kt'] + 1) * 128]
                            rhs = qT[pb_:pb_ + 64, c, r * L + v['lo'] * 128: r * L + (v['hi'] + 1) * 128]
                            mm(psf(sb + par)[:, 0:nq], lhsT, rhs, True, True)
                        s_in = PSt[:, sb:sb + 2, 0:nq]
                        p_out = PT[slot][:, :, 0:nq]
                        OP('act', 'activation', out=p_out, in_=s_in, func=AF.Exp, scale=0.125, reads=[s_in], writes=[p_out])
                        mk = mflat[:, v['mcol']:v['mcol'] + nq].unsqueeze(1).to_broadcast([128, 2, nq])
                        OP('dve', 'tensor_tensor', p_out, p_out, mk, ALU.mult,
                           reads=[p_out, mflat[:, v['mcol']:v['mcol'] + nq]], writes=[p_out])
                    pendq.append((v, obanks, slot))
                    if v['fin'] is not None:
                        open_ob -= set(obanks)
                    flush(LAG)
                flush(0)
            if debug:
                dma('sp', dbg['d_acc'], acc, 'dbg')
                dma('sp', dbg['d_q'], qT, 'dbg')
                dma('sp', dbg['d_k'], kT, 'dbg')
                dma('sp', dbg['d_v'], vA, 'dbg')
            rden = view(U0, 1024, F32)
            for tg in range(NTG):
                for hs in range(4):
                    b = P.bank()
                    rhs = acc[0:65, hs, tg * 512:(tg + 1) * 512]
                    mm(PSt[0:64, b, :], sel[0:65, :], rhs, True, True)
                    OP('act', 'activation', out=rden[0:64, :], in_=PSt[0:64, b, :], func=AF.Ln, reads=[PSt[0:64, b, :]], writes=[rden[0:64, :]])
                    OP('act', 'activation', out=rden[0:64, :], in_=rden[0:64, :], func=AF.Exp, scale=-1.0, reads=[rden[0:64, :]], writes=[rden[0:64, :]])
                    o_ = yaT[0:64, hs, tg * 512:(tg + 1) * 512]
                    OP('dve', 'tensor_tensor', o_, acc[0:64, hs, tg * 512:(tg + 1) * 512], rden[0:64, :], ALU.mult,
                       reads=[acc[0:64, hs, tg * 512:(tg + 1) * 512], rden[0:64, :]], writes=[o_])

            if debug:
                dma('sp', dbg['d_yfT'], yfT, 'dbg')
                dma('sp', dbg['d_yaT'], yaT, 'dbg')
            mergedT = view(U0 + 1024, 4096, pat="p (c t) -> p c t", c=8)
            gt = [view(U0 + 5120 + i * 1024, 1024, F32) for i in range(4)]
            wfb_ = view(U0 + 9216, 2048, pat="p (k j) -> p k j", k=2)
            wab = view(U0 + 11264, 4096, pat="p (h j) -> p h j", h=4)
            dma('pool', wfb_, dr['w_fourier_branch'].rearrange("(k p) j -> p k j", p=128), 'wfa')
            dma('pool', wab[0:64], dr['w_attn_branch'].rearrange("(h e) j -> e h j", e=64), 'wab')
            def req_gates(which):
                out_ = []
                for nm_, i_ in which:
                    out_.append(wpiece(wcols('w_in', (2560 if nm_ == 'f' else 3584) + i_ * 512, 512)))
                return out_
            gcur = req_gates([('f', 0), ('a', 0), ('f', 1), ('a', 1)])
            for tg in range(NTG):
                tsl = slice(tg * 512, (tg + 1) * 512)
                wgf = [gcur[0], gcur[2]]; wga = [gcur[1], gcur[3]]
                for mc in range(8):
                    if mc == 4:
                        wmx = [wpiece(wcols('w_mix_out', i * 512, 512)) for i in range(2)]
                    b1 = P.bank()
                    for dc in range(8):
                        mm(psf(b1), wgf[mc // 4][:, dc, (mc % 4) * 128:(mc % 4 + 1) * 128], hT[:, dc, tsl], dc == 0, dc == 7)
                    b2 = P.bank()
                    for dc in range(8):
                        mm(psf(b2), wga[mc // 4][:, dc, (mc % 4) * 128:(mc % 4 + 1) * 128], hT[:, dc, tsl], dc == 0, dc == 7)
                    if mc in (1, 2, 3, 4):
                        norm_pop()
                    b3 = P.bank()
                    for j in range(2):
                        mm(psf(b3), wfb_[:, j, mc * 128:(mc + 1) * 128], yfT[:, j, tsl], j == 0, j == 1)
                    b4 = P.bank()
                    for hs in range(4):
                        mm(psf(b4), wab[0:64, hs, mc * 128:(mc + 1) * 128], yaT[0:64, hs, tsl], hs == 0, hs == 3)
                    OP('act', 'activation', out=gt[0], in_=psf(b1), func=AF.Tanh, scale=0.5, reads=[psf(b1)], writes=[gt[0]])
                    OP('act', 'activation', out=gt[1], in_=psf(b2), func=AF.Tanh, scale=0.5, reads=[psf(b2)], writes=[gt[1]])
                    OP('dve', 'scalar_tensor_tensor', gt[2], gt[0], 1.0, psf(b3), ALU.add, ALU.mult,
                       reads=[gt[0], psf(b3)], writes=[gt[2]])
                    OP('dve', 'scalar_tensor_tensor', gt[3], gt[1], 1.0, psf(b4), ALU.add, ALU.mult,
                       reads=[gt[1], psf(b4)], writes=[gt[3]])
                    OP('pool', 'tensor_tensor', mergedT[:, mc, :], gt[2], gt[3], ALU.add,
                       reads=[gt[2], gt[3]], writes=[mergedT[:, mc, :]])
                if tg + 1 < NTG:
                    gnext = req_gates([('f', 0), ('a', 0), ('f', 1)])
                else:
                    wkp_pre = [wpiece(wcols('w_ckv', half * 512, 512)) for half in range(2)]
                    wvp_pre = [wpiece(wcols('w_ckv', 1024, 512))]
                for tt in range(4):
                    t = tg * 4 + tt
                    dma('sp', xres[:, t, :], xs[t * 128:(t + 1) * 128, :], 'xr%d' % (t % 4))
                    for nh in range(2):
                        b = P.bank()
                        for fc in range(8):
                            mm(psf(b), mergedT[:, fc, tt * 128:(tt + 1) * 128], wmx[nh][:, fc, :], fc == 0, fc == 7)
                        o_ = xres[:, t, nh * 512:(nh + 1) * 512]
                        OP('dve', 'scalar_tensor_tensor', o_, psf(b), 0.5, o_, ALU.mult, ALU.add,
                           reads=[psf(b), o_], writes=[o_])
                if tg + 1 < NTG:
                    gnext += req_gates([('a', 1)])
                    gcur = gnext
                else:
                    wvp_pre.append(wpiece(wcols('w_ckv', 1024 + 512, 512)))
                norm_defer([(xres[:, tg * 4 + tt, :], 1, hT[:, :, (tg * 4 + tt) * 128:(tg * 4 + tt + 1) * 128]) for tt in range(4)])

            if debug:
                dma('sp', dbg['d_x1'], xres, 'dbg')
            KT = view(U0, 2048, pat="p (c m) -> p c m", c=8)
            Vm = view(U0 + 2048, 2048, pat="p (m f) -> p m f", m=2)
            qcT = view(U0 + 4096, 4096, pat="p (c t) -> p c t", c=8)
            PTc = [view(U0 + 8192 + i * 1024, 1024, pat="p (m t) -> p m t", m=2) for i in range(2)]
            rdc = view(U0 + 10240, 1024, F32)
            ocT = view(U0 + 11264, 4096, pat="p (c t) -> p c t", c=8)
            for half in range(2):
                wkp = wkp_pre[half]
                for fc4 in range(4):
                    fc = half * 4 + fc4
                    b = P.bank()
                    for dc in range(8):
                        mm(psf(b)[:, 0:256], wkp[:, dc, fc4 * 128:(fc4 + 1) * 128], memnT[:, dc, :], dc == 0, dc == 7)
                    OP('act', 'activation', out=KT[:, fc, :], in_=psf(b)[:, 0:256], func=AF.Copy,
                       reads=[psf(b)[:, 0:256]], writes=[KT[:, fc, :]])
            for half in range(2):
                wvp = wvp_pre[half]
                for mt in range(2):
                    b = P.bank()
                    for dc in range(8):
                        mm(psf(b), memnT[:, dc, mt * 128:(mt + 1) * 128], wvp[:, dc, :], dc == 0, dc == 7)
                    o_ = Vm[:, mt, half * 512:(half + 1) * 512]
                    OP('dve', 'tensor_copy', o_, psf(b), reads=[psf(b)], writes=[o_])
            norm_pop(4)
            wcq = [wpiece(wcols('w_cq', i * 512, 512)) for i in range(2)]
            wco = [wpiece(wcols('w_co', i * 512, 512)) for i in range(2)]
            wup_pre = [wpiece(wcols('w_up', 0, 512))]
            wdn_pre = []
            for tg in range(NTG):
                tsl = slice(tg * 512, (tg + 1) * 512)
                for fc in range(8):
                    b = P.bank()
                    for dc in range(8):
                        mm(psf(b), wcq[fc // 4][:, dc, (fc % 4) * 128:(fc % 4 + 1) * 128], hT[:, dc, tsl], dc == 0, dc == 7)
                    if fc in (1, 2, 3, 4):
                        norm_pop()
                    if fc % 2 == 0:
                        OP('act', 'activation', out=qcT[:, fc, :], in_=psf(b), func=AF.Copy, reads=[psf(b)], writes=[qcT[:, fc, :]])
                    else:
                        OP('dve', 'tensor_copy', qcT[:, fc, :], psf(b), reads=[psf(b)], writes=[qcT[:, fc, :]])
                if tg == NTG - 1:
                    wup_pre.append(wpiece(wcols('w_up', 512, 512)))
                    wdn_pre.append(req_dn(0, 0))
                def c_scores(h):
                    pt = PTc[h % 2]
                    for mt in range(2):
                        b = P.bank()
                        for ec in range(2):
                            mm(psf(b), KT[:, 2 * h + ec, mt * 128:(mt + 1) * 128], qcT[:, 2 * h + ec, :], ec == 0, ec == 1)
                        OP('act', 'activation', out=pt[:, mt, :], in_=psf(b), func=AF.Exp, scale=1.0 / 16.0,
                           reads=[psf(b)], writes=[pt[:, mt, :]])

                def c_rest(h):
                    pt = PTc[h % 2]
                    bd_ = P.bank()
                    for mt in range(2):
                        mm(psf(bd_), ones, pt[:, mt, :], mt == 0, mt == 1)
                    OP('act', 'activation', out=rdc, in_=psf(bd_), func=AF.Ln, reads=[psf(bd_)], writes=[rdc])
                    OP('act', 'activation', out=rdc, in_=rdc, func=AF.Exp, scale=-1.0, reads=[rdc], writes=[rdc])
                    for ec in range(2):
                        b = P.bank()
                        for mt in range(2):
                            mm(psf(b), Vm[:, mt, h * 256 + ec * 128: h * 256 + (ec + 1) * 128], pt[:, mt, :], mt == 0, mt == 1)
                        o_ = ocT[:, 2 * h + ec, :]
                        OP('dve', 'tensor_tensor', o_, psf(b), rdc, ALU.mult, reads=[psf(b), rdc], writes=[o_])

                c_scores(0)
                for h in range(4):
                    if h + 1 < 4:
                        c_scores(h + 1)
                    c_rest(h)
                for tt in range(4):
                    t = tg * 4 + tt
                    for nh in range(2):
                        b = P.bank()
                        for fc in range(8):
                            mm(psf(b), ocT[:, fc, tt * 128:(tt + 1) * 128], wco[nh][:, fc, :], fc == 0, fc == 7)
                        o_ = xres[:, t, nh * 512:(nh + 1) * 512]
                        OP('dve', 'tensor_tensor', o_, psf(b), o_, ALU.add, reads=[psf(b), o_], writes=[o_])
                norm_defer([(xres[:, tg * 4 + tt, :], 3, hT[:, :, (tg * 4 + tt) * 128:(tg * 4 + tt + 1) * 128]) for tt in range(4)])
            if debug:
                dma('sp', dbg['d_x2'], xres, 'dbg')

            a2T = [view(U0 + i * 4096, 4096, pat="p (c t) -> p c t", c=8) for i in range(2)]
            rl = [view(U0 + 8192 + i * 512, 512) for i in range(2)]
            ostage = [view(U0 + 9216 + i * 2048, 2048, F32) for i in range(2)]
            gfin = view(U0 + 13312, 2048, F32)
            OP('sp', 'dma_start', out=gfin, in_=dr['norm_final'].partition_broadcast(128),
               reads=[], writes=[gfin], dma_key='gfin')
            rlc = 0
            def up(tg, wup):
                tsl = slice(tg * 512, (tg + 1) * 512)
                for fc in range(8):
                    b = P.bank()
                    for dc in range(8):
                        mm(psf(b), wup[fc // 4][:, dc, (fc % 4) * 128:(fc % 4 + 1) * 128], hT[:, dc, tsl], dc == 0, dc == 7)
                    r_ = rl[rlc[0] % 2]
                    rlc[0] += 1
                    OP('act', 'activation', out=r_, in_=psf(b), func=AF.Relu, reads=[psf(b)], writes=[r_])
                    o_ = a2T[tg % 2][:, fc, :]
                    OP('pool', 'tensor_tensor', o_, r_, r_, ALU.mult, reads=[r_], writes=[o_])

            def down(tg, wdn, last):
                for tt in range(4):
                    t = tg * 4 + tt
                    for nh in range(2):
                        b = P.bank()
                        for fc in range(8):
                            mm(psf(b), a2T[tg % 2][:, fc, tt * 128:(tt + 1) * 128], wdn[nh][:, fc, :], fc == 0, fc == 7)
                        o_ = xres[:, t, nh * 512:(nh + 1) * 512]
                        OP('dve', 'tensor_tensor', o_, psf(b), o_, ALU.add, reads=[psf(b), o_], writes=[o_])
                    if last:
                        xt = xres[:, t, :]
                        kk_ = nt_ctr[0] % 4
                        nt_ctr[0] += 1
                        rstd = rms_stats(xt, xn[kk_])
                        os_ = ostage[t % 2]
                        OP('dve', 'scalar_tensor_tensor', os_, xt, rstd, gfin, ALU.mult, ALU.mult,
                           reads=[xt, rstd, gfin], writes=[os_])
                        dma('sp', out_d[b_][t * 128:(t + 1) * 128, :], os_, 'out%d' % (t % 2))

            rlc = [0]
            wup = wup_pre
            wdn = [wdn_pre[0], req_dn(0, 1)]
            for p_ in range(4):
                nxt_up = None; nxt_dn = None
                up(0, wup)
                norm_pop(4)
                for tg in range(NTG):
                    if tg + 1 < NTG:
                        up(tg + 1, wup)
                    if tg == NTG - 2 and p_ < 3:
                        nxt_up = req_up(p_ + 1)
                        nxt_dn = [req_dn(p_ + 1, 0)]
                    down(tg, wdn, p_ == 3)
                    if p_ == 3 and tg == 1 and b_ + 1 < nseq:
                        for t_ in range(4):
                            dma('sp', xin[t_], dr['x'][b_ + 1][t_ * 128:(t_ + 1) * 128, :], 'xin%d' % t_)
                        prefetched.add(b_ + 1)
                if p_ < 3:
                    nxt_dn.append(req_dn(p_ + 1, 1))
                    wup, wdn = nxt_up, nxt_dn

        P.emit(sem_alloc)
    return nc


_CACHE = {}


def kernel(**inputs):
    nseq = SEQ_PER_CORE
    consts = host_constants()
    x = np.ascontiguousarray(inputs['x'], dtype=np.float32)
    mem = np.ascontiguousarray(inputs['mem'], dtype=np.float32)
    B = x.shape[0]
    n_cores = B // nseq
    if 'nc' not in _CACHE:
        _CACHE['nc'] = build_program(nseq)
    nc = _CACHE['nc']
    shared = {}
    for name, shp in W_SPECS:
        a = np.asarray(inputs[name], dtype=np.float32)
        shared[name] = np.ascontiguousarray(a.reshape(shp))
    for name, shp, dt in CONST_SPECS:
        shared[name] = consts[name]
    in_maps = []
    for c in range(n_cores):
        m = dict(shared)
        m['x'] = x[c * nseq:(c + 1) * nseq]
        m['mem'] = mem[c * nseq:(c + 1) * nseq]
        in_maps.append(m)
    res = run_bass_kernel_spmd(nc, in_maps, core_ids=list(range(n_cores)))
    out = np.concatenate([np.asarray(r['out']) for r in res.results], axis=0)
    return out.astype(np.float32)
```
